# Optimizing a Trainium2 kernel written in Bass

```python
import jax
import jax.numpy as jnp
from jax import lax
import numpy as np

D_MODEL = 1024
BATCH = 2
SEQ = 8192
DEPTH = 1

GRID_W = 64
CTX_LEN = 256
N_DIR = 2
RET_HEADS = 4
RET_DK = 128
RET_DV = 128
RET_WIDTH = RET_HEADS * RET_DV
RET_CHUNK = 128
RWKV_HEADS = 8
RWKV_N = 64
RWKV_WIDTH = RWKV_HEADS * RWKV_N
DECAY_LORA = 32
AAA_LORA = 32
GATE_LORA = 96
D_FF = 2816
N_MOD = 9
ROPE_BASE = 10000.0
NORM_EPS = 1e-6
RET_GN_EPS = 1e-5
RWKV_GN_EPS = 64e-5
IN_SPLITS = (RET_WIDTH, 2 * RET_WIDTH, 3 * RET_WIDTH, 4 * RET_WIDTH,
             4 * RET_WIDTH + RWKV_WIDTH, 4 * RET_WIDTH + 2 * RWKV_WIDTH, 4 * RET_WIDTH + 3 * RWKV_WIDTH,
             4 * RET_WIDTH + 3 * RWKV_WIDTH + D_MODEL)
IN_WIDTH = 4 * RET_WIDTH + 3 * RWKV_WIDTH + 2 * D_MODEL

kernel_name = 'hybrid_retention_rwkv7_macaron_dit'


def rms_norm(x, g):
    xf = x.astype(jnp.float32)
    y = xf * lax.rsqrt(jnp.mean(xf * xf, axis=-1, keepdims=True) + NORM_EPS)
    return (y * g.astype(jnp.float32)).astype(x.dtype)


def modulate(t, shift, scale):
    return t * (1.0 + scale) + shift


def swiglu(t, w_gate, w_up, w_down):
    return (jax.nn.silu(t @ w_gate) * (t @ w_up)) @ w_down


def split_heads(t, n_heads):
    return t.reshape(t.shape[:-1] + (n_heads, t.shape[-1] // n_heads))


def head_norm(y, eps):
    mu = jnp.mean(y, axis=-1, keepdims=True)
    var = jnp.mean(jnp.square(y - mu), axis=-1, keepdims=True)
    return (y - mu) * lax.rsqrt(var + eps)


def centred_shift(z):
    zp = jnp.pad(z, ((0, 0), (1, 1), (0, 0)))
    return 0.5 * (zp[:, :-2] + zp[:, 2:])


def axial_rope(t, rows, cols):
    half = t.shape[-1] // 2
    n_freq = half // 2
    inv = ROPE_BASE ** (-jnp.arange(n_freq, dtype=jnp.float32) / n_freq)

    def rot(z, pos):
        ang = pos.astype(jnp.float32)[:, None] * inv[None, :]
        cos = jnp.cos(ang)[None, :, None, :]
        sin = jnp.sin(ang)[None, :, None, :]
        z1, z2 = z[..., :n_freq], z[..., n_freq:]
        return jnp.concatenate([z1 * cos - z2 * sin, z1 * sin + z2 * cos], axis=-1)

    return jnp.concatenate([rot(t[..., :half], rows), rot(t[..., half:], cols)], axis=-1)


def retention_chunks(q, k, v, log_gamma, state0):
    nd, bsz, length, nh, _ = q.shape
    n_chunks = length // RET_CHUNK
    idx = jnp.arange(RET_CHUNK)
    diff = idx[:, None] - idx[None, :]
    mask = diff[None] >= jnp.arange(nd)[:, None, None]
    lg = log_gamma[:, :, None, None]
    expo = jnp.where(mask[:, None], diff.astype(jnp.float32)[None, None], 0.0)
    intra = jnp.where(mask[:, None], jnp.exp(expo * lg), 0.0)
    idx_f = idx.astype(jnp.float32)
    dq = jnp.exp((idx_f + 1.0)[None, None, :] * log_gamma[:, :, None])
    dkk = jnp.exp((RET_CHUNK - 1.0 - idx_f)[None, None, :] * log_gamma[:, :, None])
    dq_t = jnp.swapaxes(dq, 1, 2)[:, None, :, :, None]
    dk_t = jnp.swapaxes(dkk, 1, 2)[:, None, :, :, None]
    dchunk = jnp.exp(RET_CHUNK * log_gamma)[:, None, :, None, None]

    def to_chunks(t):
        return jnp.moveaxis(t.reshape(nd, bsz, n_chunks, RET_CHUNK, nh, t.shape[-1]), 2, 0)

    def step(state, inp):
        qc, kc, vc = inp
        s = jnp.einsum('zbihd,zbjhd->zbhij', qc, kc) * intra[:, None]
        y = (jnp.einsum('zbhij,zbjhe->zbihe', s, vc)
             + jnp.einsum('zbihd,zbhde->zbihe', qc * dq_t, state))
        state = dchunk * state + jnp.einsum('zbjhd,zbjhe->zbhde', kc * dk_t, vc)
        return state, y

    state, ys = lax.scan(step, state0, (to_chunks(q), to_chunks(k), to_chunks(v)))
    y = jnp.moveaxis(ys, 0, 2).reshape(nd, bsz, length, nh, v.shape[-1])
    return y, state


def retention_segment(q, k, v, log_gamma, state0):
    both = lambda t: jnp.stack([t, jnp.flip(t, 1)])
    y, state = retention_chunks(both(q), both(k), both(v), log_gamma, state0)
    return y[0] + jnp.flip(y[1], 1), state


def retention_readout(y, g_raw):
    bsz, length = y.shape[:2]
    return jax.nn.silu(g_raw.astype(jnp.float32)) * head_norm(y, RET_GN_EPS).reshape(bsz, length, RET_WIDTH)


def rwkv_prepare(u, pr, pk, pv, mu_rkv, mu_x, w0, w1, w2, a0, a1, a2, k_k, k_a):
    f32 = jnp.float32
    u, pr, pk, pv = (t.astype(f32) for t in (u, pr, pk, pv))
    r = pr + mu_rkv[0] * (centred_shift(pr) - pr)
    k = pk + mu_rkv[1] * (centred_shift(pk) - pk)
    v = pv + mu_rkv[2] * (centred_shift(pv) - pv)
    du = centred_shift(u) - u
    xw = u + mu_x[0] * du
    xa = u + mu_x[1] * du
    xg = u + mu_x[2] * du
    w_pre = w0[:, None, None, :] + jnp.einsum('zblr,zrc->zblc', jnp.tanh(jnp.einsum('bld,zdr->zblr', xw, w1)), w2)
    decay = jnp.exp(-jnp.exp(-jax.nn.softplus(-w_pre) - 0.5))
    a = jax.nn.sigmoid(a0[:, None, None, :] + jnp.einsum('zblr,zrc->zblc', jnp.einsum('bld,zdr->zblr', xa, a1), a2))
    kk = split_heads(k * k_k, RWKV_HEADS)
    kk = (kk / jnp.maximum(jnp.sqrt(jnp.sum(kk * kk, axis=-1, keepdims=True)), 1e-12)).reshape(k.shape)
    k_eff = k[None] * (1.0 + (a - 1.0) * k_a)
    b_vec = kk[None] * a
    per_dir = lambda t: split_heads(jnp.stack([t[0], jnp.flip(t[1], 1)]), RWKV_HEADS)
    shared = lambda t: split_heads(jnp.stack([t, jnp.flip(t, 1)]), RWKV_HEADS)
    scan_in = (shared(r), per_dir(decay), per_dir(k_eff), shared(v), shared(-kk), per_dir(b_vec))
    return scan_in, r, k_eff[0], v, xg


def rwkv7_scan(r, w, k, v, a, b, state0):
    is_fwd = (jnp.arange(N_DIR) == 0)[:, None, None, None]

    def step(s, inp):
        rt, wt, kt, vt, at, bt = inp
        sa = jnp.einsum('zbhvk,zbhk->zbhv', s, at)
        s_new = s * wt[..., None, :] + sa[..., :, None] * bt[..., None, :] + vt[..., :, None] * kt[..., None, :]
        y = jnp.where(is_fwd, jnp.einsum('zbhvk,zbhk->zbhv', s_new, rt), jnp.einsum('zbhvk,zbhk->zbhv', s, rt))
        return s_new, y

    xs = tuple(jnp.moveaxis(t, 2, 0) for t in (r, w, k, v, a, b))
    state, ys = lax.scan(step, state0, xs)
    y = jnp.moveaxis(ys, 0, 2)
    return y[0] + jnp.flip(y[1], 1), state


def rwkv_readout(y, r, k, v, xg, g1, g2, r_k, ln_w, ln_b):
    bsz, length = y.shape[:2]
    yn = head_norm(y, RWKV_GN_EPS).reshape(bsz, length, RWKV_WIDTH) * ln_w + ln_b
    bonus = jnp.sum(split_heads(r * k, RWKV_HEADS) * r_k, axis=-1, keepdims=True) * split_heads(v, RWKV_HEADS)
    g = jax.nn.sigmoid(xg @ g1) @ g2
    return (yn + bonus.reshape(bsz, length, RWKV_WIDTH)) * g


def parallel_mixer(uc, ul, rows, cols, w_in, ret_decay_logit, w_ret_o, mu_rkv, mu_x, w0, w1, w2, a0, a1, a2,
                   g1, g2, k_k, k_a, r_k, ln_w, ln_b, w_rwkv_o, w_out, with_ctx_out):
    f32 = jnp.float32
    bsz = ul.shape[0]
    dt = ul.dtype
    c_q, c_k, c_v, c_g, c_r, c_rk, c_rv, c_gret, c_grw = jnp.split(uc @ w_in, IN_SPLITS, axis=-1)
    l_q, l_k, l_v, l_g, l_r, l_rk, l_rv, l_gret, l_grw = jnp.split(ul @ w_in, IN_SPLITS, axis=-1)

    log_gamma = jax.nn.log_sigmoid(ret_decay_logit.astype(f32))
    k_scale = RET_DK ** -0.5
    hd = lambda t: split_heads(t.astype(f32), RET_HEADS)
    ret_state0 = jnp.zeros((N_DIR, bsz, RET_HEADS, RET_DK, RET_DV), f32)
    y_ret_c, ret_state_c = retention_segment(hd(c_q), hd(c_k) * k_scale, hd(c_v), log_gamma, ret_state0)
    y_ret_l, _ = retention_segment(axial_rope(hd(l_q), rows, cols), axial_rope(hd(l_k), rows, cols) * k_scale,
                                   hd(l_v), log_gamma, ret_state_c)

    lora = (mu_rkv, mu_x, w0, w1, w2, a0, a1, a2, k_k, k_a)
    scan_c, r_c, k_c, v_c, xg_c = rwkv_prepare(uc, c_r, c_rk, c_rv, *lora)
    scan_l, r_l, k_l, v_l, xg_l = rwkv_prepare(ul, l_r, l_rk, l_rv, *lora)
    rw_state0 = jnp.zeros((N_DIR, bsz, RWKV_HEADS, RWKV_N, RWKV_N), f32)
    y_rw_c, rw_state_c = rwkv7_scan(*scan_c, rw_state0)
    y_rw_l, _ = rwkv7_scan(*scan_l, rw_state_c)

    def merge(y_ret, g_ret, y_rw, r, k, v, xg, gate_ret, gate_rw):
        ret_out = retention_readout(y_ret, g_ret).astype(dt) @ w_ret_o
        rw_out = rwkv_readout(y_rw, r, k, v, xg, g1, g2, r_k, ln_w, ln_b).astype(dt) @ w_rwkv_o
        return (jax.nn.sigmoid(gate_ret) * ret_out + jax.nn.sigmoid(gate_rw) * rw_out) @ w_out

    out_l = merge(y_ret_l, l_g, y_rw_l, r_l, k_l, v_l, xg_l, l_gret, l_grw)
    if with_ctx_out:
        return out_l, merge(y_ret_c, c_g, y_rw_c, r_c, k_c, v_c, xg_c, c_gret, c_grw)
    return out_l, None


def setup_inputs(seed: int = 0) -> dict:
    key = jax.random.key(seed)
    keys = iter(jax.random.split(key, 48))
    f32 = jnp.float32
    nrm = lambda shape, scale: scale * jax.random.normal(next(keys), shape, f32)
    uni = lambda shape, lo, hi: jax.random.uniform(next(keys), shape, f32, lo, hi)
    L, D = DEPTH, D_MODEL
    p = 2.0 ** (-5.0 - jnp.arange(RET_HEADS, dtype=f32))
    return {
        'x': nrm((BATCH, SEQ, D), 1.0),
        'c': nrm((BATCH, D), 1.0),
        'ctx': nrm((BATCH, CTX_LEN, D), 1.0),
        'c_ctx': nrm((D,), 1.0),
        'w_mod': nrm((L, D, N_MOD * D), 0.5 * D ** -0.5),
        'b_mod': nrm((L, N_MOD * D), 0.02),
        'g_ffn1': 1.0 + nrm((L, D), 0.02),
        'ffn1_w_gate': nrm((L, D, D_FF), D ** -0.5),
        'ffn1_w_up': nrm((L, D, D_FF), D ** -0.5),
        'ffn1_w_down': nrm((L, D_FF, D), D_FF ** -0.5),
        'g_mix': 1.0 + nrm((L, D), 0.02),
        'w_in': nrm((L, D, IN_WIDTH), D ** -0.5),
        'ret_decay_logit': jnp.log((1.0 - p) / p)[None, None, :] + nrm((L, N_DIR, RET_HEADS), 0.1),
        'w_ret_o': nrm((L, RET_WIDTH, D), RET_WIDTH ** -0.5),
        'rwkv_mu_rkv': uni((L, 3, RWKV_WIDTH), 0.0, 1.0),
        'rwkv_mu_x': uni((L, 3, D), 0.0, 1.0),
        'rwkv_w0': uni((L, N_DIR, RWKV_WIDTH), -6.0, 1.0),
        'rwkv_w1': nrm((L, N_DIR, D, DECAY_LORA), D ** -0.5),
        'rwkv_w2': nrm((L, N_DIR, DECAY_LORA, RWKV_WIDTH), 0.5 * DECAY_LORA ** -0.5),
        'rwkv_a0': nrm((L, N_DIR, RWKV_WIDTH), 0.5),
        'rwkv_a1': nrm((L, N_DIR, D, AAA_LORA), D ** -0.5),
        'rwkv_a2': nrm((L, N_DIR, AAA_LORA, RWKV_WIDTH), 0.5 * AAA_LORA ** -0.5),
        'rwkv_g1': nrm((L, D, GATE_LORA), D ** -0.5),
        'rwkv_g2': nrm((L, GATE_LORA, RWKV_WIDTH), GATE_LORA ** -0.5),
        'rwkv_k_k': 0.85 + nrm((L, RWKV_WIDTH), 0.05),
        'rwkv_k_a': 1.0 + nrm((L, RWKV_WIDTH), 0.05),
        'rwkv_r_k': nrm((L, RWKV_HEADS, RWKV_N), 0.1),
        'rwkv_ln_w': 1.0 + nrm((L, RWKV_WIDTH), 0.02),
        'rwkv_ln_b': nrm((L, RWKV_WIDTH), 0.02),
        'w_rwkv_o': nrm((L, RWKV_WIDTH, D), RWKV_WIDTH ** -0.5),
        'w_out': nrm((L, D, D), D ** -0.5),
        'g_ffn2': 1.0 + nrm((L, D), 0.02),
        'ffn2_w_gate': nrm((L, D, D_FF), D ** -0.5),
        'ffn2_w_up': nrm((L, D, D_FF), D ** -0.5),
        'ffn2_w_down': nrm((L, D_FF, D), D_FF ** -0.5),
        'g_final': 1.0 + nrm((D,), 0.02),
    }


def reference(x, c, ctx, c_ctx, w_mod, b_mod, g_ffn1, ffn1_w_gate, ffn1_w_up, ffn1_w_down, g_mix, w_in,
              ret_decay_logit, w_ret_o, rwkv_mu_rkv, rwkv_mu_x, rwkv_w0, rwkv_w1, rwkv_w2, rwkv_a0, rwkv_a1,
              rwkv_a2, rwkv_g1, rwkv_g2, rwkv_k_k, rwkv_k_a, rwkv_r_k, rwkv_ln_w, rwkv_ln_b, w_rwkv_o, w_out,
              g_ffn2, ffn2_w_gate, ffn2_w_up, ffn2_w_down, g_final):
    n_rows = x.shape[1] // GRID_W
    rows = jnp.repeat(jnp.arange(n_rows), GRID_W)
    cols = jnp.tile(jnp.arange(GRID_W), n_rows)
    h, hc = x, ctx
    for layer in range(DEPTH):
        last = layer == DEPTH - 1
        mod_l = jnp.split((jax.nn.silu(c) @ w_mod[layer] + b_mod[layer])[:, None, :], N_MOD, axis=-1)
        mod_c = jnp.split(jax.nn.silu(c_ctx) @ w_mod[layer] + b_mod[layer], N_MOD, axis=-1)
        ffn1 = (ffn1_w_gate[layer], ffn1_w_up[layer], ffn1_w_down[layer])
        ffn2 = (ffn2_w_gate[layer], ffn2_w_up[layer], ffn2_w_down[layer])
        h = h + 0.5 * mod_l[2] * swiglu(modulate(rms_norm(h, g_ffn1[layer]), mod_l[0], mod_l[1]), *ffn1)
        hc = hc + 0.5 * mod_c[2] * swiglu(modulate(rms_norm(hc, g_ffn1[layer]), mod_c[0], mod_c[1]), *ffn1)
        ul = modulate(rms_norm(h, g_mix[layer]), mod_l[3], mod_l[4])
        uc = modulate(rms_norm(hc, g_mix[layer]), mod_c[3], mod_c[4])
        mix_l, mix_c = parallel_mixer(
            uc, ul, rows, cols, w_in[layer], ret_decay_logit[layer], w_ret_o[layer],
            rwkv_mu_rkv[layer], rwkv_mu_x[layer], rwkv_w0[layer], rwkv_w1[layer], rwkv_w2[layer],
            rwkv_a0[layer], rwkv_a1[layer], rwkv_a2[layer], rwkv_g1[layer], rwkv_g2[layer],
            rwkv_k_k[layer], rwkv_k_a[layer], rwkv_r_k[layer], rwkv_ln_w[layer], rwkv_ln_b[layer],
            w_rwkv_o[layer], w_out[layer], not last)
        h = h + mod_l[5] * mix_l
        h = h + 0.5 * mod_l[8] * swiglu(modulate(rms_norm(h, g_ffn2[layer]), mod_l[6], mod_l[7]), *ffn2)
        if not last:
            hc = hc + mod_c[5] * mix_c
            hc = hc + 0.5 * mod_c[8] * swiglu(modulate(rms_norm(hc, g_ffn2[layer]), mod_c[6], mod_c[7]), *ffn2)
    return rms_norm(h, g_final)
```

```python
import os
import contextlib
import numpy as np
import ml_dtypes
import concourse.bass as bass
import concourse.mybir as mybir
from concourse.bass_utils import run_bass_kernel_spmd

F32 = mybir.dt.float32
BF16 = mybir.dt.bfloat16
AF = mybir.ActivationFunctionType
ALU = mybir.AluOpType
AX = mybir.AxisListType
NDSEM = 92


class Res:
    def __init__(self, name, t=None, kind="sb"):
        self.name = name
        self.t = t
        self.kind = kind
        self.last_w = None
        self.reads = {}
        self.dsem = None

    def __getitem__(self, idx):
        return View(self, self.t[idx])


class View:
    def __init__(self, res, ap):
        self.res = res
        self.ap = ap

    def __getitem__(self, idx):
        return View(self.res, self.ap[idx])

    def bc(self, shape):
        return View(self.res, self.ap.to_broadcast(list(shape)))

    def re(self, pat, **kw):
        return View(self.res, self.ap.rearrange(pat, **kw))


def _ap(x):
    return x.ap if isinstance(x, View) else x


class Sched:
    ENG = ("pe", "act", "dve", "pool", "sp")

    def __init__(self, nc):
        self.nc = nc
        self.cnt = {e: 0 for e in self.ENG}
        self.waited = {e: {} for e in self.ENG}
        self.sems = {}
        self.dma_cnt = {}
        self.n_dsem = 0
        self.eobj = {"pe": nc.tensor, "act": nc.scalar, "dve": nc.vector, "pool": nc.gpsimd, "sp": nc.sync}
        self.sem_cms = []
        for e in self.ENG:
            if e != "sp":
                cm = nc.semaphore("s_eng_" + e)
                self.sems[("eng", e)] = cm.__enter__()
                self.sem_cms.append(cm)
        for i in range(NDSEM):
            cm = nc.semaphore("s_dma_%d" % i)
            self.sems[("dma", i)] = cm.__enter__()
            self.sem_cms.append(cm)
            self.dma_cnt[("dma", i)] = 0
        self.stack = contextlib.ExitStack()
        self.nalloc = 0
        self.nins = {e: 0 for e in self.ENG}
        self.incpts = {e: [] for e in self.ENG}
        self.last_h = {e: None for e in self.ENG}

    def sb(self, name, shape, dt=F32, stack=None):
        self.nalloc += 1
        t = (stack or self.stack).enter_context(self.nc.sbuf_tensor("%s_%d" % (name, self.nalloc), list(shape), dt))
        return Res(name, t)

    def ps(self, name, shape, dt=F32, stack=None):
        self.nalloc += 1
        t = (stack or self.stack).enter_context(self.nc.psum_tensor("%s_%d" % (name, self.nalloc), list(shape), dt))
        return Res(name, t, "ps")

    def dram(self, name, ap):
        return Res(name, ap, "dram")

    @contextlib.contextmanager
    def phase(self):
        old = self.stack
        with contextlib.ExitStack() as st:
            self.stack = st
            try:
                yield st
            finally:
                self.barrier()
                self.stack = old

    def _deps(self, eng, reads, writes):
        deps = {}

        def add(ev):
            if ev is None:
                return
            k, v = ev
            if eng == "pe" and k == ("eng", "pe"):
                return
            if k[0] == "dma":
                v = self.dma_cnt[k]
            if deps.get(k, 0) < v:
                deps[k] = v
        for r in reads:
            add(r.last_w)
        for w in writes:
            add(w.last_w)
            for k, v in w.reads.items():
                add((k, v))
        out = []
        wd = self.waited[eng]
        for k, v in deps.items():
            if k[0] == "eng":
                v = self._resolve(k[1], v)
            if wd.get(k, 0) < v:
                wd[k] = v
                out.append((k, v))
        return out

    def _resolve(self, e, idx):
        pts = self.incpts[e]
        lo, hi = 0, len(pts)
        while lo < hi:
            mid = (lo + hi) // 2
            if pts[mid][0] >= idx:
                hi = mid
            else:
                lo = mid + 1
        if lo < len(pts):
            return pts[lo][1]
        cnt = len(pts) + 1
        self.last_h[e].then_inc(self.sems[("eng", e)], 1)
        pts.append((self.nins[e], cnt))
        return cnt

    def _record(self, ev, reads, writes):
        k, v = ev
        for r in reads:
            if r.reads.get(k, 0) < v:
                r.reads[k] = v
        for w in writes:
            w.last_w = ev
            w.reads = {}

    def op(self, eng, fn, reads=(), writes=()):
        reads = [r.res if isinstance(r, View) else r for r in reads if r is not None and not isinstance(r, (int, float))]
        writes = [w.res if isinstance(w, View) else w for w in writes if w is not None]
        waits = self._deps(eng, reads, writes)
        self.cnt[eng] += 1
        self.nins[eng] += 1
        ev = (("eng", eng), self.nins[eng])
        eo = self.eobj[eng]
        for k, v in waits:
            eo.wait_ge(self.sems[k], v)
        self.last_h[eng] = fn(eo)
        self._record(ev, reads, writes)

    def dma(self, out, in_, queue="sp", **kw):
        ro, ri = out.res, in_.res
        owner = ri if ro.kind == "dram" else ro
        if owner.dsem is None:
            owner.dsem = ("dma", self.n_dsem % NDSEM)
            self.n_dsem += 1
        waits = self._deps(queue, [ri], [ro])
        self.dma_cnt[owner.dsem] += 16
        ev = (owner.dsem, self.dma_cnt[owner.dsem])
        eo = self.eobj[queue]
        for k, v in waits:
            eo.wait_ge(self.sems[k], v)
        eo.dma_start(out=out.ap, in_=in_.ap, **kw).then_inc(self.sems[owner.dsem], 16)
        self._record(ev, [ri], [ro])

    def collective(self, out_res, in_res, in_ap, out_ap, kind="AllGather", ncores=8):
        if out_res.dsem is None:
            out_res.dsem = ("dma", self.n_dsem % NDSEM)
            self.n_dsem += 1
        key = out_res.dsem
        waits = self._deps("pool", [in_res], [out_res])
        eo = self.eobj["pool"]
        for k, v in waits:
            eo.wait_ge(self.sems[k], v)
        self.dma_cnt[key] += 1
        eo.collective_compute(kind, ALU.bypass, replica_groups=[list(range(ncores))], ins=[in_ap], outs=[out_ap]).then_inc(self.sems[key], 1)
        self._record((key, self.dma_cnt[key]), [in_res], [out_res])

    def barrier(self):
        evs = [(("eng", g), self._resolve(g, self.nins[g])) for g in self.ENG if g != "sp" and self.nins[g] > 0]
        evs += [(k, v) for k, v in self.dma_cnt.items() if v > 0]
        for e in self.ENG:
            wd = self.waited[e]
            for k, v in evs:
                if wd.get(k, 0) < v:
                    wd[k] = v
                    self.eobj[e].wait_ge(self.sems[k], v)

    def finish(self):
        self.barrier()
        self.stack.close()
        for cm in reversed(self.sem_cms):
            cm.__exit__(None, None, None)

    def mm(self, out, lhsT, rhs, start=True, stop=True, **kw):
        self.op("pe", lambda e: e.matmul(out.ap, lhsT=lhsT.ap, rhs=rhs.ap, start=start, stop=stop, **kw),
                reads=[lhsT, rhs], writes=[out])

    def tr(self, out, in_, ident):
        self.op("pe", lambda e: e.transpose(out=out.ap, in_=in_.ap, identity=ident.ap), reads=[in_, ident], writes=[out])

    def act(self, out, in_, func, scale=1.0, bias=0.0, accum_out=None, eng="act"):
        kw = {}
        if accum_out is not None:
            kw["accum_out"] = accum_out.ap
        self.op(eng, lambda e: e.activation(out=out.ap, in_=in_.ap, func=func, scale=_ap(scale), bias=_ap(bias), **kw),
                reads=[in_, scale, bias], writes=[out, accum_out])

    def tt(self, eng, out, in0, in1, op):
        self.op(eng, lambda e: e.tensor_tensor(out=out.ap, in0=in0.ap, in1=in1.ap, op=op), reads=[in0, in1], writes=[out])

    def ts(self, eng, out, in0, s1, s2=None, op0=ALU.mult, op1=None):
        if op1 is None:
            self.op(eng, lambda e: e.tensor_scalar(out=out.ap, in0=in0.ap, scalar1=_ap(s1), scalar2=None, op0=op0),
                    reads=[in0, s1], writes=[out])
        else:
            self.op(eng, lambda e: e.tensor_scalar(out=out.ap, in0=in0.ap, scalar1=_ap(s1), scalar2=_ap(s2), op0=op0, op1=op1),
                    reads=[in0, s1, s2], writes=[out])

    def stt(self, eng, out, in0, scalar, in1, op0, op1):
        self.op(eng, lambda e: e.scalar_tensor_tensor(out=out.ap, in0=in0.ap, scalar=_ap(scalar), in1=in1.ap, op0=op0, op1=op1),
                reads=[in0, scalar, in1], writes=[out])

    def copy(self, eng, out, in_):
        if eng == "act":
            self.op(eng, lambda e: e.copy(out=out.ap, in_=in_.ap), reads=[in_], writes=[out])
        else:
            self.op(eng, lambda e: e.tensor_copy(out=out.ap, in_=in_.ap), reads=[in_], writes=[out])

    def memset(self, eng, out, val):
        self.op(eng, lambda e: e.memset(out.ap, val), writes=[out])


D_FF = 2816
NFC = 22
EPS = 1e-6
UC = 2308
LAT0 = 1
CTX0 = 2051


def fm(v):
    return np.ascontiguousarray(np.asarray(v, np.float32).reshape(-1, 128).T)


def prep_common(inp):
    d = {}
    d["ident"] = np.eye(128, dtype=np.float32)
    d["w_mod"] = inp["w_mod"][0]
    d["bmodT"] = fm(inp["b_mod"][0])
    d["gvec"] = np.ascontiguousarray(np.stack([fm(inp["g_ffn1"][0]), fm(inp["g_mix"][0]), fm(inp["g_ffn2"][0])], 1))
    d["f1_wg"] = inp["ffn1_w_gate"][0]
    d["f1_wu"] = inp["ffn1_w_up"][0]
    d["f1_wd"] = inp["ffn1_w_down"][0]
    return d


def prep_core(inp, r):
    b, j = r // 4, r % 4
    x = inp["x"][b]
    lo = j * 2048
    d = {}
    d["x_lat"] = np.ascontiguousarray(x[lo:lo + 2048])
    halo = np.zeros((2, 1024), np.float32)
    hm = np.zeros((128, 2), np.float32)
    if j > 0:
        halo[0] = x[lo - 1]
        hm[:, 0] = 1
    if j < 3:
        halo[1] = x[lo + 2048]
        hm[:, 1] = 1
    d["x_halo"] = halo
    d["halo_mask"] = hm
    d["x_ctx"] = np.ascontiguousarray(inp["ctx"][b])
    d["cT"] = np.ascontiguousarray(np.stack([fm(inp["c"][b]), fm(inp["c_ctx"])], -1))
    return d


IN_SHAPES = {
    "ident": [128, 128], "w_mod": [1024, 9216], "bmodT": [128, 72], "gvec": [128, 3, 8],
    "f1_wg": [1024, D_FF], "f1_wu": [1024, D_FF], "f1_wd": [D_FF, 1024],
    "x_lat": [2048, 1024], "x_halo": [2, 1024], "halo_mask": [128, 2], "x_ctx": [256, 1024], "cT": [128, 8, 2],
}


class Ctx:
    pass


def mod_phase(S, D, C):
    C.ident = S.sb("ident", [128, 128])
    C.identb = S.sb("identb", [128, 128], BF16)
    S.dma(C.ident[:], D["ident"][:])
    S.copy("dve", C.identb[:], C.ident[:])
    C.modT = S.sb("modT", [128, 72, 2])
    C.gv = S.sb("gv", [128, 3, 8])
    S.dma(C.gv[:], D["gvec"][:])
    with S.phase():
        cT = S.sb("cT", [128, 8, 2])
        sc = S.sb("sc", [128, 8, 2])
        bm = S.sb("bm", [128, 72])
        S.dma(cT[:], D["cT"][:])
        S.dma(bm[:], D["bmodT"][:])
        S.act(sc[:], cT[:], AF.Silu)
        psm = S.ps("psm", [128, 144])
        wb = [S.sb("wm%d" % i, [128, 8, 1024]) for i in range(2)]
        wsrc = D["w_mod"].t.rearrange("(kc p) c -> p kc c", p=128)
        for m in range(9):
            w = wb[m % 2]
            for kc in range(8):
                S.dma(w[:, kc, :], View(D["w_mod"], wsrc[:, kc, m * 1024:(m + 1) * 1024]))
            for oc in range(8):
                o = m * 8 + oc
                for kc in range(8):
                    S.mm(psm[:, 2 * o:2 * o + 2], w[:, kc, oc * 128:(oc + 1) * 128], sc[:, kc, :],
                         start=(kc == 0), stop=(kc == 7))
        psv = psm[:].re("p (o c) -> p o c", c=2)
        for col in range(2):
            S.tt("dve", C.modT[:, :, col], psv[:, :, col], bm[:], ALU.add)
    def mk_gs(name, gi, scale_idx):
        t = S.sb(name, [128, 8, 2])
        for col in range(2):
            S.stt("dve", t[:, :, col], C.modT[:, scale_idx * 8:(scale_idx + 1) * 8, col], 1.0, C.gv[:, gi, :], ALU.add, ALU.mult)
        return t
    C.gs1 = mk_gs("gs1", 0, 1)
    C.gsm = mk_gs("gsm", 1, 4)
    C.gs2 = mk_gs("gs2", 2, 7)


def bcast_rows(S, C, name, idx, col, mul):
    G = S.sb(name, [128, 1024])
    with S.phase():
        tmp = S.sb("bct", [128, 128])
        psb = [S.ps("psb%d" % i, [128, 512]) for i in range(2)]
        for dc in range(8):
            S.copy("dve", tmp[:], C.modT[:, idx * 8 + dc, col:col + 1].bc([128, 128]))
            S.mm(psb[dc // 4][:, (dc % 4) * 128:(dc % 4 + 1) * 128], tmp[:], C.ident[:])
        for hf in range(2):
            S.act(G[:, hf * 512:(hf + 1) * 512], psb[hf][:], AF.Copy, scale=mul)
    return G


def norm_to_T(S, C, xt, rows, gs, sh_idx, col, pst, dst_fn, scr):
    ss, rs, junk, xn = scr
    S.memset("pool", ss[:rows, :], 0.0)
    S.act(junk[:rows, :], xt[:rows, :], AF.Square, accum_out=ss[:rows, :])
    S.ts("dve", rs[:rows, :], ss[:rows, :], 1.0 / 1024, EPS, ALU.mult, ALU.add)
    S.act(rs[:rows, :], rs[:rows, :], AF.Sqrt)
    S.op("dve", lambda e: e.reciprocal(out=rs.t[:rows, :], in_=rs.t[:rows, :]), reads=[rs], writes=[rs])
    S.act(xn[:rows, :], xt[:rows, :], AF.Copy, scale=rs[:rows, 0:1])
    for dc in range(8):
        S.tr(pst[:, dc, :rows], xn[:rows, dc * 128:(dc + 1) * 128], C.identb[:rows, :rows])
    for dc in range(8):
        o = dst_fn(dc)
        if dc % 2 == 0:
            S.ts("dve", o, pst[:, dc, :rows], gs[:, dc, col:col + 1], C.modT[:, sh_idx * 8 + dc, col:col + 1], ALU.mult, ALU.add)
        else:
            S.act(o, pst[:, dc, :rows], AF.Identity, scale=gs[:, dc, col:col + 1], bias=C.modT[:, sh_idx * 8 + dc, col:col + 1])


def ffn_phase1(S, D, C):
    with S.phase():
        _ffn_phase1(S, D, C)
        S.dma(D["ulT_sp"][:], C.ulT[:])


def _ffn_phase1(S, D, C):
    C.ulT = S.sb("ulT", [128, 8, UC], BF16)
    S.memset("pool", C.ulT[:, :, 2050:2051], 0.0)
    S.memset("pool", C.ulT[:, :, 2307:2308], 0.0)
    G1 = [bcast_rows(S, C, "G1l", 2, 0, 0.5), bcast_rows(S, C, "G1c", 2, 1, 0.5)]
    tiles = []
    for i in range(16):
        tiles.append(dict(src=("x_lat", i * 128), rows=128, mc=0, ucol=LAT0 + i * 128, hrow=i * 128))
    for i in range(2):
        tiles.append(dict(src=("x_ctx", i * 128), rows=128, mc=1, ucol=CTX0 + i * 128, hrow=None))
    tiles.append(dict(src=("x_halo", 0), rows=2, mc=0, ucol=None, hrow=None))
    sbs = [tiles[0:6], tiles[6:12], tiles[12:19]]
    with S.phase():
        wd = S.sb("wd", [128, NFC, 1024], BF16)
        stg = [S.sb("stg%d" % i, [128, 8, 128]) for i in range(4)]
        wgb = [S.sb("wgb%d" % i, [128, 8, 128], BF16) for i in range(4)]
        wdsrc = D["f1_wd"].t.rearrange("(fc p) d -> p fc d", p=128)
        wdst = [S.sb("wdst%d" % i, [128, 1024]) for i in range(2)]
        for fc in range(NFC):
            S.dma(wdst[fc % 2][:], View(D["f1_wd"], wdsrc[:, fc, :]))
            S.copy("pool", wd[:, fc, :], wdst[fc % 2][:])
        u1 = S.sb("u1", [128, 8, 770], BF16)
        actT = S.sb("actT", [128, NFC, 770], BF16)
        xt = [S.sb("xt%d" % i, [128, 1024]) for i in range(2)]
        ht = [S.sb("ht%d" % i, [128, 1024]) for i in range(2)]
        scr = (S.sb("ss", [128, 1]), S.sb("rs", [128, 1]), S.sb("junk", [128, 1024], BF16), S.sb("xn", [128, 1024], BF16))
        sg = [S.sb("sg%d" % i, [128, 512]) for i in range(2)]
        tmpd = [S.sb("tmpd%d" % i, [128, 512]) for i in range(2)]
        pst = S.ps("pst", [128, 8, 128], BF16)
        psg = [S.ps("psg%d" % i, [128, 512]) for i in range(2)]
        psu = [S.ps("psu%d" % i, [128, 512]) for i in range(2)]
        psd = [S.ps("psd%d" % i, [128, 512]) for i in range(2)]
        wgsrc = D["f1_wg"].t.rearrange("(kc p) f -> p kc f", p=128)
        wusrc = D["f1_wu"].t.rearrange("(kc p) f -> p kc f", p=128)
        nload = 0
        for sbi, sb in enumerate(sbs):
            col = 0
            for ti, t in enumerate(sb):
                x = xt[ti % 2]
                rows = t["rows"]
                S.dma(x[:rows, :], D[t["src"][0]][t["src"][1]:t["src"][1] + rows, :])
                c0 = col
                norm_to_T(S, C, x, rows, C.gs1, 0, t["mc"], pst, lambda dc, c0=c0, rows=rows: u1[:, dc, c0:c0 + rows], scr)
                t["c0"] = c0
                col += rows
            ncol = col
            groups = [(g0, min(512, ncol - g0)) for g0 in range(0, ncol, 512)]
            it = 0
            for fc in range(NFC):
                k = nload % 2
                nload += 1
                S.dma(stg[2 * k][:], View(D["f1_wg"], wgsrc[:, :, fc * 128:(fc + 1) * 128]))
                S.dma(stg[2 * k + 1][:], View(D["f1_wu"], wusrc[:, :, fc * 128:(fc + 1) * 128]))
                S.copy("pool", wgb[2 * k][:], stg[2 * k][:])
                S.copy("pool", wgb[2 * k + 1][:], stg[2 * k + 1][:])
                for (g0, gn) in groups:
                    pg, pu = psg[it % 2], psu[it % 2]
                    for kc in range(8):
                        S.mm(pg[:, :gn], wgb[2 * k][:, kc, :], u1[:, kc, g0:g0 + gn], start=(kc == 0), stop=(kc == 7))
                    for kc in range(8):
                        S.mm(pu[:, :gn], wgb[2 * k + 1][:, kc, :], u1[:, kc, g0:g0 + gn], start=(kc == 0), stop=(kc == 7))
                    s = sg[it % 2]
                    S.act(s[:, :gn], pg[:, :gn], AF.Silu)
                    S.tt("dve", actT[:, fc, g0:g0 + gn], s[:, :gn], pu[:, :gn], ALU.mult)
                    it += 1
            for ti, t in enumerate(sb):
                rows, c0 = t["rows"], t["c0"]
                x = xt[ti % 2]
                h = ht[ti % 2]
                S.dma(x[:rows, :], D[t["src"][0]][t["src"][1]:t["src"][1] + rows, :])
                for hf in range(2):
                    for fc in range(NFC):
                        S.mm(psd[hf][:rows, :], actT[:, fc, c0:c0 + rows], wd[:, fc, hf * 512:(hf + 1) * 512],
                             start=(fc == 0), stop=(fc == NFC - 1))
                for hf in range(2):
                    S.tt("dve", tmpd[hf][:rows, :], psd[hf][:rows, :], G1[t["mc"]][:rows, hf * 512:(hf + 1) * 512], ALU.mult)
                    S.tt("pool", h[:rows, hf * 512:(hf + 1) * 512], tmpd[hf][:rows, :], x[:rows, hf * 512:(hf + 1) * 512], ALU.add)
                if t["hrow"] is not None:
                    S.dma(D["h_sp"][t["hrow"]:t["hrow"] + 128, :], h[:, :])
                if t["ucol"] is not None:
                    uc = t["ucol"]
                    dst = lambda dc, uc=uc, rows=rows: C.ulT[:, dc, uc:uc + rows]
                else:
                    dst = lambda dc: C.ulT[:, dc, 0:2050:2049]
                norm_to_T(S, C, h, rows, C.gsm, 3, t["mc"], pst, dst, scr)
        hm = S.sb("hm", [128, 2])
        S.dma(hm[:], D["halo_mask"][:])
        S.ts("dve", C.ulT[:, :, 0], C.ulT[:, :, 0], hm[:, 0:1])
        S.ts("dve", C.ulT[:, :, 2049], C.ulT[:, :, 2049], hm[:, 1:2])


K_SCALE = 128 ** -0.5
QC, KC, VC, GC = 0, 512, 1024, 1536
RC, RKC, RVC, GRET, GRW = 2048, 2560, 3072, 3584, 4608


def rope_partner():
    d = np.arange(128)
    return np.where(d % 64 < 32, d + 32, d - 32)


def prep_ret_common(inp):
    d = {}
    w_in = inp["w_in"][0]
    d["w_in"] = w_in
    perm = rope_partner()
    cols = []
    for base in (QC, KC):
        for h in range(4):
            cols.append(base + h * 128 + perm)
    d["w_rot"] = np.ascontiguousarray(w_in[:, np.concatenate(cols)])
    d["lgb"] = np.ascontiguousarray(np.broadcast_to(inp["ret_decay_logit"][0].reshape(1, 8), (128, 8))).astype(np.float32)
    i = np.arange(128, dtype=np.float32)
    j = i[:, None]
    ii = i[None, :]
    cst = np.zeros((128, 6, 128), np.float32)
    cst[:, 0] = ii + 1
    cst[:, 1] = 128 - ii
    cst[:, 2] = np.maximum(ii - j, 0)
    cst[:, 3] = np.maximum(j - ii, 0)
    cst[:, 4] = (j <= ii)
    cst[:, 5] = (j > ii)
    d["rcst"] = cst
    d["pcol"] = np.stack([127 - i, i], -1).astype(np.float32)
    return d


def prep_ret_core(inp, r):
    j = r % 4
    t = j * 2048 + np.arange(2048)
    rows, cols = t // 64, t % 64
    inv = (10000.0 ** (-np.arange(32, dtype=np.float32) / 32)).astype(np.float32)
    dd = np.arange(128)
    pos = np.where(dd[:, None] < 64, rows[None, :], cols[None, :]).astype(np.float32)
    ang = pos * inv[dd % 32][:, None]
    sgn = np.where(dd % 64 < 32, -1.0, 1.0)[:, None]
    return {"cosT": np.cos(ang).astype(np.float32), "sinT": (np.sin(ang) * sgn).astype(np.float32)}


IN_SHAPES.update({"w_in": [1024, 5632], "w_rot": [1024, 1024], "lgb": [128, 8], "rcst": [128, 6, 128], "pcol": [128, 2],
                  "cosT": [128, 2048], "sinT": [128, 2048]})


def load_w_chunk(S, Dres, src3, c0, stg, wb):
    S.dma(stg[:], View(Dres, src3[:, :, c0:c0 + 128]))
    S.copy("pool", wb[:], stg[:])


def ret_tables(S, D, C):
    C.lg = S.sb("lg", [128, 8])
    C.gC = S.sb("gC", [128, 8])
    C.Mc = S.sb("Mc", [128, 4, 128], BF16)
    C.dqT = S.sb("dqT", [128, 4, 2, 128], BF16)
    C.dk = S.sb("dk", [128, 4, 2])
    with S.phase():
        lgb = S.sb("lgb", [128, 8])
        cst = S.sb("rcst", [128, 6, 128])
        pc = S.sb("pcol", [128, 2])
        S.dma(lgb[:], D["lgb"][:])
        S.dma(cst[:], D["rcst"][:])
        S.dma(pc[:], D["pcol"][:])
        S.act(lgb[:], lgb[:], AF.Sigmoid)
        S.act(C.lg[:], lgb[:], AF.Ln)
        S.act(C.gC[:], C.lg[:], AF.Exp, scale=128.0)
        t1 = S.sb("t1", [128, 128])
        t2 = S.sb("t2", [128, 128])
        for h in range(4):
            S.act(t1[:], cst[:, 2, :], AF.Exp, scale=C.lg[:, h:h + 1])
            S.act(t2[:], cst[:, 3, :], AF.Exp, scale=C.lg[:, 4 + h:5 + h])
            S.tt("dve", t1[:], t1[:], cst[:, 4, :], ALU.mult)
            S.tt("dve", t2[:], t2[:], cst[:, 5, :], ALU.mult)
            S.tt("dve", C.Mc[:, h, :], t1[:], t2[:], ALU.add)
            S.act(C.dqT[:, h, 0, :], cst[:, 0, :], AF.Exp, scale=C.lg[:, h:h + 1])
            S.act(C.dqT[:, h, 1, :], cst[:, 1, :], AF.Exp, scale=C.lg[:, 4 + h:5 + h])
            S.act(C.dk[:, h, 0:1], pc[:, 0:1], AF.Exp, scale=C.lg[:, h:h + 1])
            S.act(C.dk[:, h, 1:2], pc[:, 1:2], AF.Exp, scale=C.lg[:, 4 + h:5 + h])


def retention_phase1(S, D, C):
    with S.phase():
        C.ulT = S.sb("ulT", [128, 8, UC], BF16)
        S.dma(C.ulT[:], D["ulT_sp"][:])
        _retention_phase1(S, D, C)


def _retention_phase1(S, D, C):
    ret_tables(S, D, C)
    win3 = D["w_in"].t.rearrange("(kc p) c -> p kc c", p=128)
    wrot3 = D["w_rot"].t.rearrange("(kc p) c -> p kc c", p=128)
    with S.phase():
        qT = S.sb("qT", [128, 4, 2048], BF16)
        kT = S.sb("kT", [128, 4, 2048], BF16)
        kcT = S.sb("kcT", [128, 4, 256], BF16)
        vtok = S.sb("vtok", [128, 18, 512], BF16)
        S_at = S.sb("S_at", [128, 16, 2, 4, 128], BF16)
        with S.phase():
            cos = S.sb("cos", [128, 2048])
            sin = S.sb("sin", [128, 2048])
            cosk = S.sb("cosk", [128, 2048])
            sink = S.sb("sink", [128, 2048])
            S.dma(cos[:], D["cosT"][:])
            S.dma(sin[:], D["sinT"][:])
            S.act(cosk[:], cos[:], AF.Copy, scale=K_SCALE)
            S.act(sink[:], sin[:], AF.Copy, scale=K_SCALE)
            stg = [S.sb("rstg%d" % i, [128, 8, 128]) for i in range(4)]
            wb = [S.sb("rwb%d" % i, [128, 8, 128], BF16) for i in range(4)]
            psA = [S.ps("psA%d" % i, [128, 512]) for i in range(2)]
            psB = [S.ps("psB%d" % i, [128, 512]) for i in range(2)]
            t1 = [S.sb("rt1_%d" % i, [128, 512]) for i in range(2)]
            t2 = [S.sb("rt2_%d" % i, [128, 512]) for i in range(2)]
            it = 0
            nl = 0
            for which, dst, ct, st in (("q", qT, cos, sin), ("k", kT, cosk, sink)):
                base = QC if which == "q" else KC
                rbase = 0 if which == "q" else 512
                for h in range(4):
                    k2 = nl % 2
                    nl += 1
                    load_w_chunk(S, D["w_in"], win3, base + h * 128, stg[2 * k2], wb[2 * k2])
                    load_w_chunk(S, D["w_rot"], wrot3, rbase + h * 128, stg[2 * k2 + 1], wb[2 * k2 + 1])
                    for g in range(4):
                        pa, pb = psA[it % 2], psB[it % 2]
                        c0 = LAT0 + g * 512
                        for kc in range(8):
                            S.mm(pa[:], wb[2 * k2][:, kc, :], C.ulT[:, kc, c0:c0 + 512], start=(kc == 0), stop=(kc == 7))
                        for kc in range(8):
                            S.mm(pb[:], wb[2 * k2 + 1][:, kc, :], C.ulT[:, kc, c0:c0 + 512], start=(kc == 0), stop=(kc == 7))
                        a, b_ = t1[it % 2], t2[it % 2]
                        S.tt("dve", a[:], pa[:], ct[:, g * 512:(g + 1) * 512], ALU.mult)
                        S.tt("dve", b_[:], pb[:], st[:, g * 512:(g + 1) * 512], ALU.mult)
                        S.tt("pool", dst[:, h, g * 512:(g + 1) * 512], a[:], b_[:], ALU.add)
                        it += 1
                    if which == "k":
                        pa = psA[it % 2]
                        for kc in range(8):
                            S.mm(pa[:, :256], wb[2 * k2][:, kc, :], C.ulT[:, kc, CTX0:CTX0 + 256], start=(kc == 0), stop=(kc == 7))
                        S.act(kcT[:, h, :], pa[:, :256], AF.Copy, scale=K_SCALE)
                        it += 1
            wv = S.sb("wv", [128, 8, 512], BF16)
            for c4 in range(4):
                load_w_chunk(S, D["w_in"], win3, VC + c4 * 128, stg[c4 % 4], wb[c4 % 4])
                S.copy("pool", wv[:, :, c4 * 128:(c4 + 1) * 128], wb[c4 % 4][:])
            for t in range(18):
                c0 = LAT0 + t * 128 if t < 16 else CTX0 + (t - 16) * 128
                pa = psA[t % 2]
                for kc in range(8):
                    S.mm(pa[:], C.ulT[:, kc, c0:c0 + 128], wv[:, kc, :], start=(kc == 0), stop=(kc == 7))
                S.act(vtok[:, t, :], pa[:], AF.Copy)
        S.dma(D["qT_sp"][:], qT[:])
        with S.phase():
            pk = [S.ps("pkt%d" % i, [128, 4, 128], BF16) for i in range(2)]
            pkv = [S.ps("pkv%d" % i, [128, 4, 128]) for i in range(2)]
            ktk = [S.sb("ktk%d" % i, [128, 4, 128], BF16) for i in range(2)]
            Sst = [S.sb("Sst%d" % i, [128, 4, 128]) for i in range(2)]
            Sctx = [S.sb("Sctx%d" % i, [128, 4, 128]) for i in range(2)]

            def kv(src, c, tile, dr, n):
                p = pk[n % 2]
                for h in range(4):
                    S.tr(p[:, h, :], src[:, h, c * 128:(c + 1) * 128], C.identb[:])
                kk_ = ktk[n % 2]
                for h in range(4):
                    if h % 2 == 0:
                        S.ts("dve", kk_[:, h, :], p[:, h, :], C.dk[:, h, dr:dr + 1])
                    else:
                        S.act(kk_[:, h, :], p[:, h, :], AF.Copy, scale=C.dk[:, h, dr:dr + 1])
                pv_ = pkv[n % 2]
                for h in range(4):
                    S.mm(pv_[:, h, :], kk_[:, h, :], vtok[:, tile, h * 128:(h + 1) * 128])
                return pv_

            n = 0
            for dr in range(2):
                for (src, nchunk, tile0, St, is_lat) in ((kcT, 2, 16, Sctx[dr], False), (kT, 16, 0, Sst[dr], True)):
                    S.memset("pool", St[:], 0.0)
                    order = range(nchunk) if dr == 0 else range(nchunk - 1, -1, -1)
                    for c in order:
                        if is_lat:
                            S.copy("pool", S_at[:, c, dr, :, :], St[:])
                        pv_ = kv(src, c, tile0 + c, dr, n)
                        n += 1
                        for h in range(4):
                            S.stt("dve", St[:, h, :], St[:, h, :], C.gC[:, dr * 4 + h:dr * 4 + h + 1], pv_[:, h, :], ALU.mult, ALU.add)
                S.dma(D["st_ret"][dr, 0], Sst[dr][:])
                S.dma(D["st_ret"][dr, 1], Sctx[dr][:])
        with S.phase():
            pS = [S.ps("pS%d" % i, [128, 4, 128]) for i in range(2)]
            pY = [S.ps("pY%d" % i, [128, 4, 128]) for i in range(2)]
            sm = [S.sb("sm%d" % i, [128, 4, 128], BF16) for i in range(2)]
            qp = [S.sb("qp%d" % i, [128, 4, 2, 128], BF16) for i in range(2)]
            yt = [S.sb("yt%d" % i, [128, 4, 128]) for i in range(2)]
            for c in range(16):
                cs = slice(c * 128, (c + 1) * 128)
                p = pS[c % 2]
                for h in range(4):
                    S.mm(p[:, h, :], kT[:, h, cs], qT[:, h, cs])
                S.tt("dve", sm[c % 2][:], p[:], C.Mc[:], ALU.mult)
                for dr in range(2):
                    S.tt("pool", qp[c % 2][:, :, dr, :], qT[:, :, cs], C.dqT[:, :, dr, :], ALU.mult)
                py = pY[c % 2]
                for h in range(4):
                    S.mm(py[:, h, :], sm[c % 2][:, h, :], vtok[:, c, h * 128:(h + 1) * 128], start=True, stop=False)
                    S.mm(py[:, h, :], qp[c % 2][:, h, 0, :], S_at[:, c, 0, h, :], start=False, stop=False)
                    S.mm(py[:, h, :], qp[c % 2][:, h, 1, :], S_at[:, c, 1, h, :], start=False, stop=True)
                S.act(yt[c % 2][:], py[:], AF.Copy)
                S.dma(D["yret_sp"][c * 128:(c + 1) * 128, :], yt[c % 2][:].re("p h e -> p (h e)"))


C0W = -float(np.exp(-0.5))
BLK = 128


def prep_rw_common(inp):
    d = {}
    g = lambda n: np.asarray(inp[n][0], np.float32)
    fm4 = lambda v: np.ascontiguousarray(v.reshape(4, 128).T)
    tab = np.zeros((128, 64), np.float32)
    mu = g("rwkv_mu_rkv")
    tab[:, 0:4], tab[:, 4:8], tab[:, 8:12] = fm4(mu[0]), fm4(mu[1]), fm4(mu[2])
    tab[:, 12:16], tab[:, 16:20] = fm4(g("rwkv_w0")[0]), fm4(g("rwkv_w0")[1])
    tab[:, 20:24], tab[:, 24:28] = fm4(g("rwkv_a0")[0]), fm4(g("rwkv_a0")[1])
    tab[:, 28:32] = fm4(g("rwkv_k_k"))
    tab[:, 32:36] = fm4(g("rwkv_k_a"))
    tab[:, 36:40] = fm4(g("rwkv_r_k").reshape(512))
    mx = g("rwkv_mu_x")
    for i in range(3):
        tab[:, 40 + 8 * i:48 + 8 * i] = fm(mx[i])
    d["rwtab"] = tab
    d["w1s"] = np.ascontiguousarray(np.concatenate([g("rwkv_w1")[0], g("rwkv_w1")[1]], 1))
    d["a1s"] = np.ascontiguousarray(np.concatenate([g("rwkv_a1")[0], g("rwkv_a1")[1]], 1))
    d["g1"] = g("rwkv_g1")
    d["w2s"] = np.ascontiguousarray(g("rwkv_w2").transpose(1, 0, 2))
    d["a2s"] = np.ascontiguousarray(g("rwkv_a2").transpose(1, 0, 2))
    d["g2"] = g("rwkv_g2")
    p = np.arange(64)[:, None]
    f = np.arange(64)[None, :]
    cm = np.zeros((64, 4, 8, 64), np.float32)
    cm[:, 0] = (p < f)[:, None, :]
    cm[:, 1] = (p <= f)[:, None, :]
    cm[:, 2] = (p > f)[:, None, :]
    cm[:, 3] = (p == f)[:, None, :]
    d["cmask"] = cm
    bo = np.zeros((128, 128), np.float32)
    bo[:64, :64] = 1
    bo[64:, 64:] = 1
    d["bones"] = bo
    return d


IN_SHAPES.update({"rwtab": [128, 64], "w1s": [1024, 64], "a1s": [1024, 64], "g1": [1024, 96], "w2s": [32, 2, 512],
                  "a2s": [32, 2, 512], "g2": [96, 512], "cmask": [64, 4, 8, 64], "bones": [128, 128]})


def rwkv_phase1(S, D, C):
    win3 = D["w_in"].t.rearrange("(kc p) c -> p kc c", p=128)
    with S.phase():
        tab = S.sb("rwtab", [128, 64])
        S.dma(tab[:], D["rwtab"][:])
        omka = S.sb("omka", [128, 4])
        S.ts("dve", omka[:], tab[:, 32:36], -1.0, 1.0, ALU.mult, ALU.add)
        cmask = S.sb("cmask", [64, 4, 8, 64])
        S.dma(cmask[:], D["cmask"][:])
        M_LT, M_LE, M_GT, M_EQ = (cmask[:, i] for i in range(4))
        bones = S.sb("bones", [128, 128])
        S.dma(bones[:], D["bones"][:])

        def load_bf(name, src, shape):
            b = S.sb(name, shape, BF16)
            with S.phase():
                st = S.sb(name + "_f", shape)
                S.dma(st[:], src)
                S.copy("pool", b[:], st[:])
            return b
        w1b = load_bf("w1b", View(D["w1s"], D["w1s"].t.rearrange("(kc p) r -> p kc r", p=128)), [128, 8, 64])
        a1b = load_bf("a1b", View(D["a1s"], D["a1s"].t.rearrange("(kc p) r -> p kc r", p=128)), [128, 8, 64])
        g1b = load_bf("g1b", View(D["g1"], D["g1"].t.rearrange("(kc p) r -> p kc r", p=128)), [128, 8, 96])
        w2b = load_bf("w2b", D["w2s"][:], [32, 2, 512])
        a2b = load_bf("a2b", D["a2s"][:], [32, 2, 512])
        g2b = load_bf("g2b", D["g2"][:], [96, 512])
        wr = S.sb("wr", [128, 8, 1536], BF16)
        with S.phase():
            wst = [S.sb("wst%d" % i, [128, 8, 128]) for i in range(2)]
            for c12 in range(12):
                S.dma(wst[c12 % 2][:], View(D["w_in"], win3[:, :, RC + c12 * 128:RC + (c12 + 1) * 128]))
                S.copy("pool", wr[:, :, c12 * 128:(c12 + 1) * 128], wst[c12 % 2][:])

        NB = BLK
        W = NB + 2
        class Slot:
            pass
        slots = []
        for dr in range(2):
            s = Slot()
            s.dr = dr
            for nm in ("bt", "kt", "EI", "v"):
                setattr(s, nm, S.sb("%s%d" % (nm, dr), [128, 4, NB]))
            for nm in ("atP", "rtP"):
                setattr(s, nm, S.sb("%s%d" % (nm, dr), [128, 2, 4, NB]))
                S.memset("pool", getattr(s, nm)[:], 0.0)
            for nm in ("X", "XT", "P", "Aak", "Arb", "Ark", "X2", "XT2"):
                setattr(s, nm, S.sb("%s%d" % (nm, dr), [64, 8, 64]))
            for nm in ("Vx", "BP", "KP", "Wsb", "Usb"):
                setattr(s, nm, S.sb("%s%d" % (nm, dr), [64, 8, 128]))
            s.ysb = S.sb("ysb%d" % dr, [128, 8, 64])
            s.T = S.sb("T%d" % dr, [128, 4, 128])
            s.Tc = S.sb("Tc%d" % dr, [128, 4, 128])
            for nm in ("Vx", "BP", "KP"):
                S.memset("pool", getattr(s, nm)[:], 0.0)
            slots.append(s)
        ub = S.sb("ub", [128, 8, W], BF16)
        xb = [S.sb("xb%d" % i, [128, 8, NB], BF16) for i in range(3)]
        hw = S.sb("hw", [32, NB], BF16)
        ha = S.sb("ha", [32, NB], BF16)
        hg = S.sb("hg", [96, NB], BF16)
        F = [S.sb("F%d" % i, [128, W]) for i in range(11)]
        pp = [S.ps("pp%d" % i, [128, 512]) for i in range(2)]
        pa = [S.ps("pa%d" % i, [64, 512]) for i in range(2)]
        pw = [S.ps("pw%d" % i, [64, 8, 128]) for i in range(1)]
        py = S.ps("py", [128, 8, 64])
        pt = S.ps("pt", [128, 4, 128])

        def prep(s, c0, n, is_lat):
            dr = s.dr
            S.dma(ub[:, :, :n + 2], D["ulT_sp"][:, :, c0 - 1:c0 + n + 1])
            t_, du = xb[0], xb[1]
            S.tt("dve", t_[:, :, :n], ub[:, :, 0:n], ub[:, :, 2:n + 2], ALU.add)
            S.stt("dve", du[:, :, :n], t_[:, :, :n], 0.5, ub[:, :, 1:n + 1], ALU.mult, ALU.subtract)
            xm = xb[2]

            def mix(i):
                S.tt("dve", t_[:, :, :n], du[:, :, :n], tab[:, 40 + 8 * i:48 + 8 * i].re("p (k o) -> p k o", o=1).bc([128, 8, n]), ALU.mult)
                S.tt("pool", xm[:, :, :n], t_[:, :, :n], ub[:, :, 1:n + 1], ALU.add)
            mix(0)
            for kc in range(8):
                S.mm(pp[0][:32, :n], w1b[:, kc, dr * 32:(dr + 1) * 32], xm[:, kc, :n], start=(kc == 0), stop=(kc == 7))
            S.act(hw[:, :n], pp[0][:32, :n], AF.Tanh)
            mix(1)
            for kc in range(8):
                S.mm(pp[1][:32, :n], a1b[:, kc, dr * 32:(dr + 1) * 32], xm[:, kc, :n], start=(kc == 0), stop=(kc == 7))
            S.act(ha[:, :n], pp[1][:32, :n], AF.Copy)
            do_g = is_lat and dr == 0
            if do_g:
                mix(2)
                for kc in range(8):
                    S.mm(pp[0][:96, :n], g1b[:, kc, :], xm[:, kc, :n], start=(kc == 0), stop=(kc == 7))
                S.act(hg[:, :n], pp[0][:96, :n], AF.Sigmoid)
            blk = (c0 - LAT0) // NB if is_lat else None
            ip = 0
            for q4 in range(4):
                pr, pk_, pv_ = F[0], F[1], F[2]
                for i3, dst in enumerate((pr, pk_, pv_)):
                    p_ = pp[ip % 2]
                    ip += 1
                    for kc in range(8):
                        S.mm(p_[:, :n + 2], wr[:, kc, i3 * 512 + q4 * 128:i3 * 512 + (q4 + 1) * 128], ub[:, kc, :n + 2],
                             start=(kc == 0), stop=(kc == 7))
                    S.act(dst[:, :n + 2], p_[:, :n + 2], AF.Copy)
                outs = (F[5], F[6], s.v[:, q4, :n])
                for i3, (src, dst) in enumerate(zip((pr, pk_, pv_), outs)):
                    dv = dst if isinstance(dst, View) else dst[:, :n]
                    S.tt("pool", F[3][:, :n], src[:, 0:n], src[:, 2:n + 2], ALU.add)
                    S.stt("dve", F[4][:, :n], F[3][:, :n], 0.5, src[:, 1:n + 1], ALU.mult, ALU.subtract)
                    S.stt("dve", dv, F[4][:, :n], tab[:, 4 * i3 + q4:4 * i3 + q4 + 1], src[:, 1:n + 1], ALU.mult, ALU.add)
                r_, k_ = F[5], F[6]
                S.ts("dve", F[3][:, :n], k_[:, :n], tab[:, 28 + q4:29 + q4])
                S.tt("pool", F[4][:, :n], F[3][:, :n], F[3][:, :n], ALU.mult)
                p_ = pp[ip % 2]
                ip += 1
                S.mm(p_[:, :n], bones[:], F[4][:, :n])
                S.act(F[4][:, :n], p_[:, :n], AF.Sqrt)
                S.ts("dve", F[4][:, :n], F[4][:, :n], 1e-12, op0=ALU.max)
                S.op("dve", lambda e, a=F[4].t[:, :n]: e.reciprocal(out=a, in_=a), reads=[F[4]], writes=[F[4]])
                kk = F[7]
                S.tt("dve", kk[:, :n], F[3][:, :n], F[4][:, :n], ALU.mult)
                p_ = pp[ip % 2]
                ip += 1
                S.mm(p_[:, :n], w2b[:, dr, q4 * 128:(q4 + 1) * 128], hw[:, :n])
                lw = F[1]
                S.act(lw[:, :n], p_[:, :n], AF.Sigmoid, bias=tab[:, 12 + 4 * dr + q4:13 + 4 * dr + q4])
                S.ts("dve", lw[:, :n], lw[:, :n], C0W)
                p_ = pp[ip % 2]
                ip += 1
                S.mm(p_[:, :n], a2b[:, dr, q4 * 128:(q4 + 1) * 128], ha[:, :n])
                asg = F[0]
                S.act(asg[:, :n], p_[:, :n], AF.Sigmoid, bias=tab[:, 20 + 4 * dr + q4:21 + 4 * dr + q4])
                keff = F[2]
                S.ts("dve", keff[:, :n], asg[:, :n], tab[:, 32 + q4:33 + q4], omka[:, q4:q4 + 1], ALU.mult, ALU.add)
                S.tt("dve", keff[:, :n], keff[:, :n], k_[:, :n], ALU.mult)
                bb = F[3]
                S.tt("pool", bb[:, :n], kk[:, :n], asg[:, :n], ALU.mult)
                if do_g:
                    S.stt("dve", F[4][:, :n], r_[:, :n], tab[:, 36 + q4:37 + q4], keff[:, :n], ALU.mult, ALU.mult)
                    p_ = pp[ip % 2]
                    ip += 1
                    S.mm(p_[:, :n], bones[:], F[4][:, :n])
                    S.tt("dve", F[4][:, :n], p_[:, :n], s.v[:, q4, :n], ALU.mult)
                    S.dma(D["bonusT_sp"][:, q4, blk * NB:blk * NB + n], F[4][:, :n])
                    p_ = pp[ip % 2]
                    ip += 1
                    S.mm(p_[:, :n], g2b[:, q4 * 128:(q4 + 1) * 128], hg[:, :n])
                    S.act(F[10][:, :n], p_[:, :n], AF.Copy)
                    S.dma(D["gT_sp"][:, q4, blk * NB:blk * NB + n], F[10][:, :n])
                A_, B_ = lw, F[8]
                nch = n // 64
                va = lambda T_: T_[:, :n].re("p (c t) -> p c t", t=64)
                src_, dst_ = A_, B_
                for dsh in (1, 2, 4, 8, 16, 32):
                    a3, b3 = va(src_), va(dst_)
                    if dr == 0:
                        S.tt("pool", b3[:, :, dsh:], a3[:, :, dsh:], a3[:, :, :64 - dsh], ALU.add)
                        S.copy("pool", b3[:, :, :dsh], a3[:, :, :dsh])
                    else:
                        S.tt("pool", b3[:, :, :64 - dsh], a3[:, :, :64 - dsh], a3[:, :, dsh:], ALU.add)
                        S.copy("pool", b3[:, :, 64 - dsh:], a3[:, :, 64 - dsh:])
                    if dsh == 1:
                        src_, dst_ = B_, F[9]
                    else:
                        src_, dst_ = dst_, src_
                cumI = src_
                cumX = dst_
                S.tt("dve", cumX[:, :n], cumI[:, :n], lw[:, :n], ALU.subtract)
                S.act(s.EI[:, q4, :n], cumI[:, :n], AF.Exp)
                EX, EN = F[0], F[6]
                S.act(EX[:, :n], cumX[:, :n], AF.Exp)
                S.act(EN[:, :n], cumI[:, :n], AF.Exp, scale=-1.0)
                for e in range(2):
                    rw_ = slice(e * 64, (e + 1) * 64)
                    S.stt("dve", s.atP[rw_, e, q4, :n], kk[rw_, :n], -1.0, EX[rw_, :n], ALU.mult, ALU.mult)
                S.tt("dve", s.bt[:, q4, :n], bb[:, :n], EN[:, :n], ALU.mult)
                S.tt("pool", s.kt[:, q4, :n], keff[:, :n], EN[:, :n], ALU.mult)
                if is_lat:
                    for e in range(2):
                        rw_ = slice(e * 64, (e + 1) * 64)
                        S.tt("dve", s.rtP[rw_, e, q4, :n], r_[rw_, :n], (s.EI[rw_, q4, :n] if dr == 0 else EX[rw_, :n]), ALU.mult)

        def chain(s, m, T, nn, with_y, ychunk):
            dr = s.dr
            cs = slice(m * 64, (m + 1) * 64)
            hv = lambda arr, h: (arr[:, h % 2, h // 2, cs] if arr is s.atP or arr is s.rtP else arr[:, h // 2, cs])
            ms = M_LT if dr == 0 else M_GT
            msT = M_GT if dr == 0 else M_LT
            mr = M_LE if dr == 0 else M_GT
            f2 = lambda t_: t_[:].re("p h t -> p (h t)")
            p_ = pa[0]
            for q4 in range(4):
                S.tr(p_[:, q4 * 128:(q4 + 1) * 128], s.v[:, q4, cs], C.ident[:])
            S.copy("dve", s.Vx[:, :, 0:64], p_[:].re("p (h k) -> p h k", k=64))
            for src, dstp in ((s.bt, s.BP), (s.kt, s.KP)):
                p_ = pa[1]
                for q4 in range(4):
                    S.tr(p_[:, q4 * 128:(q4 + 1) * 128], src[:, q4, cs], C.ident[:])
                p4 = p_[:].re("p (q e k) -> p q e k", e=2, k=64)
                d4 = dstp[:].re("p (q e) k -> p q e k", e=2)
                S.copy("dve", d4[:, :, 0, 0:64], p4[:, :, 0, :])
                S.act(d4[:, :, 1, 64:128], p4[:, :, 1, :], AF.Copy)
            yield
            def amat(lhs, rhs, mask, dst, pi, eng):
                p_ = pa[pi]
                for h in range(8):
                    S.mm(p_[:, h * 64:(h + 1) * 64], hv(lhs, h), hv(rhs, h))
                S.tt(eng, f2(dst), p_[:], mask.re("p h t -> p (h t)"), ALU.mult)
            amat(s.bt, s.atP, ms, s.X, 0, "dve")
            amat(s.atP, s.bt, msT, s.XT, 1, "dve")
            amat(s.kt, s.atP, ms, s.Aak, 0, "dve")
            if with_y:
                amat(s.bt, s.rtP, mr, s.Arb, 1, "dve")
                amat(s.kt, s.rtP, mr, s.Ark, 0, "dve")
            S.tt("pool", f2(s.P), f2(s.X), M_EQ.re("p h t -> p (h t)"), ALU.add)
            yield
            X, XT, X2, XT2 = s.X, s.XT, s.X2, s.XT2
            for k in range(5):
                if k < 4:
                    for h in range(8):
                        S.mm(pa[0][:, h * 64:(h + 1) * 64], XT[:, h, :], X[:, h, :])
                for h in range(8):
                    S.mm(pa[1][:, h * 64:(h + 1) * 64], X[:, h, :], XT[:, h, :])
                if k < 4:
                    S.act(f2(X2), pa[0][:], AF.Copy)
                S.copy("dve", f2(XT2), pa[1][:])
                X, X2 = X2, X
                XT, XT2 = XT2, XT
                for h in range(8):
                    S.mm(pa[0][:, h * 64:(h + 1) * 64], XT[:, h, :], s.P[:, h, :])
                S.tt("dve", f2(s.P), f2(s.P), pa[0][:], ALU.add)
                yield
            pw_ = pw[0]
            for h in range(8):
                S.mm(pw_[:, h, :nn], hv(s.atP, h), T[:, h // 2, :nn], start=True, stop=False)
                S.mm(pw_[:, h, :nn], s.Aak[:, h, :], s.Vx[:, h, :nn], start=False, stop=True)
            S.act(s.Wsb[:, 0:4, :nn], pw_[:, 0:4, :nn], AF.Copy)
            S.copy("dve", s.Wsb[:, 4:8, :nn], pw_[:, 4:8, :nn])
            yield
            for h in range(8):
                S.mm(pw_[:, h, :nn], s.P[:, h, :], s.Wsb[:, h, :nn])
            S.act(s.Usb[:, 0:4, :nn], pw_[:, 0:4, :nn], AF.Copy)
            S.copy("dve", s.Usb[:, 4:8, :nn], pw_[:, 4:8, :nn])
            yield
            if with_y:
                for h in range(8):
                    S.mm(py[:, h, :], T[:, h // 2, :], hv(s.rtP, h), start=True, stop=False)
                    S.mm(py[:, h, :], s.Usb[:, h, :], s.Arb[:, h, :], start=False, stop=False)
                    S.mm(py[:, h, :], s.Vx[:, h, :], s.Ark[:, h, :], start=False, stop=True)
                S.act(s.ysb[:], py[:], AF.Copy)
                S.dma(D["yext_sp"][dr, ychunk], s.ysb[:].re("p h t -> p (h t)"))
            for q4 in range(4):
                S.mm(pt[:, q4, :nn], C.ident[:], T[:, q4, :nn], start=True, stop=False)
                for e in range(2):
                    h = 2 * q4 + e
                    S.mm(pt[:, q4, :nn], s.BP[:, h, :], s.Usb[:, h, :nn], start=False, stop=False)
                    S.mm(pt[:, q4, :nn], s.KP[:, h, :], s.Vx[:, h, :nn], start=False, stop=(e == 1))
            gi = m * 64 + (63 if dr == 0 else 0)
            for q4 in range(4):
                if q4 % 2 == 0:
                    S.ts("dve", T[:, q4, :nn], pt[:, q4, :nn], s.EI[:, q4, gi:gi + 1])
                else:
                    S.act(T[:, q4, :nn], pt[:, q4, :nn], AF.Copy, scale=s.EI[:, q4, gi:gi + 1])
            yield

        CH = [int(os.environ.get("CH_STOP", "1000000"))]

        def run_jobs(gens):
            gens = list(gens)
            while gens:
                if CH[0] <= 0:
                    return
                CH[0] -= 1
                nxt = []
                for g in gens:
                    try:
                        next(g)
                        nxt.append(g)
                    except StopIteration:
                        pass
                gens = nxt

        def block_job(s, c0, n, is_lat, T, nn, blk):
            nch = n // 64
            order = range(nch) if s.dr == 0 else range(nch - 1, -1, -1)
            for m in order:
                yc = (blk * (BLK // 64) + m) if is_lat else None
                yield from chain(s, m, T, nn, is_lat, yc)

        for s in slots:
            S.memset("pool", s.Tc[:], 0.0)
            S.memset("pool", s.T[:], 0.0)
            S.copy("pool", s.T[0:64, :, 64:128], C.ident[0:64, 0:64].re("p (o k) -> p o k", o=1).bc([64, 4, 64]))
            S.copy("pool", s.T[64:128, :, 64:128], C.ident[64:128, 64:128].re("p (o k) -> p o k", o=1).bc([64, 4, 64]))
        STOP = int(os.environ.get("RW_STOP", "99"))
        ncb = 256 // BLK
        if STOP == 0:
            prep(slots[0], CTX0, BLK, False)
            return
        for i in range(ncb):
            prep(slots[0], CTX0 + i * BLK, BLK, False)
            prep(slots[1], CTX0 + (ncb - 1 - i) * BLK, BLK, False)
            run_jobs([block_job(slots[0], 0, BLK, False, slots[0].Tc, 64, None),
                      block_job(slots[1], 0, BLK, False, slots[1].Tc, 64, None)])
        for s in slots:
            S.dma(D["st_rwc"][s.dr], s.Tc[:, :, 0:64])
        if STOP == 1:
            return
        nblk = 2048 // BLK
        CH[0] = int(os.environ.get("CH_LAT", "1000000"))
        for i in range(nblk):
            bf, bb_ = i, nblk - 1 - i
            prep(slots[0], LAT0 + bf * BLK, BLK, True)
            prep(slots[1], LAT0 + bb_ * BLK, BLK, True)
            if STOP == 2:
                return
            run_jobs([block_job(slots[0], LAT0 + bf * BLK, BLK, True, slots[0].T, 128, bf),
                      block_job(slots[1], LAT0 + bb_ * BLK, BLK, True, slots[1].T, 128, bb_)])
        for s in slots:
            S.dma(D["st_rw"][s.dr], s.T[:])


RET_EPS = 1e-5
RW_EPS = 64e-5


def prep_p2_common(inp):
    d = {}
    g = lambda n: np.asarray(inp[n][0], np.float32)
    d["w_ret_o"] = g("w_ret_o")
    d["w_rwkv_o"] = g("w_rwkv_o")
    d["w_out"] = g("w_out")
    d["lnwb"] = np.ascontiguousarray(np.broadcast_to(np.stack([g("rwkv_ln_w"), g("rwkv_ln_b")])[None], (128, 2, 512))).astype(np.float32)
    c = np.arange(16, dtype=np.float32)
    d["cpos"] = np.ascontiguousarray(np.broadcast_to(np.stack([128 * c, 128 * (15 - c)])[None], (128, 2, 16))).astype(np.float32)
    d["f2_wg"] = inp["ffn2_w_gate"][0]
    d["f2_wu"] = inp["ffn2_w_up"][0]
    d["f2_wd"] = inp["ffn2_w_down"][0]
    d["gfin"] = np.ascontiguousarray(np.broadcast_to(np.asarray(inp["g_final"], np.float32)[None], (128, 1024)))
    return d


def prep_p2_core(r):
    b, j = r // 4, r % 4
    m = np.zeros((128, 2, 8), np.float32)
    for rr in range(8):
        if rr // 4 == b and rr < r:
            m[:, 0, rr] = 1
        if rr // 4 == b and rr > r:
            m[:, 1, rr] = 1
    return {"cmaskr": m}


P2_SHAPES = {"w_ret_o": [512, 1024], "w_rwkv_o": [512, 1024], "w_out": [1024, 1024], "lnwb": [128, 2, 512], "cpos": [128, 2, 16],
             "f2_wg": [1024, D_FF], "f2_wu": [1024, D_FF], "f2_wd": [D_FF, 1024], "gfin": [128, 1024], "cmaskr": [128, 2, 8]}


def load_w_bf(S, Dres, src3, ncols, dst, stg):
    for i, c0 in enumerate(range(0, ncols, 128)):
        st = stg[i % len(stg)]
        kcs = dst.t.shape[1]
        S.dma(st[:, :kcs, :], View(Dres, src3[:, :, c0:c0 + 128]))
        S.copy("pool", dst[:, :, c0:c0 + 128], st[:, :kcs, :])


def compose_states(S, D, C, own_rank_dram):
    C.Sin_b = [S.sb("Sinb%d" % i, [128, 4, 128], BF16) for i in range(2)]
    C.LH = [S.sb("LH%d" % i, [128, 8, 64]) for i in range(2)]
    with S.phase():
        mk = S.sb("mk", [128, 2, 8])
        S.dma(mk[:], D["cmaskr"][:])
        G = S.sb("G2048", [128, 8])
        S.act(G[:], C.lg[:], AF.Exp, scale=2048.0)
        Gm1 = S.sb("Gm1", [128, 8])
        S.ts("dve", Gm1[:], G[:], -1.0, op0=ALU.add)
        cf = S.sb("cf", [128, 2, 8, 4])
        for dr in range(2):
            for h in range(4):
                S.ts("dve", cf[:, dr, :, h], mk[:, dr, :], Gm1[:, dr * 4 + h:dr * 4 + h + 1], 1.0, ALU.mult, ALU.add)
        Sin = [S.sb("Sin%d" % i, [128, 4, 128]) for i in range(2)]
        ld = [S.sb("sld%d" % i, [128, 4, 128]) for i in range(2)]
        tmp = S.sb("stmp", [128, 4, 128])
        n = 0
        for dr in range(2):
            S.dma(Sin[dr][:], D["own_st_ret"][dr, 1])
            order = range(8) if dr == 0 else range(7, -1, -1)
            for rr in order:
                l = ld[n % 2]
                n += 1
                S.dma(l[:], D["all_st_ret"][rr, dr, 0])
                S.ts("dve", tmp[:], l[:], mk[:, dr, rr:rr + 1])
                for h in range(4):
                    S.stt("dve", Sin[dr][:, h, :], Sin[dr][:, h, :], cf[:, dr, rr, h:h + 1], tmp[:, h, :], ALU.mult, ALU.add)
            S.copy("dve", C.Sin_b[dr][:], Sin[dr][:])
        Tin = [S.sb("Tin%d" % i, [128, 4, 64]) for i in range(2)]
        tl = [S.sb("tld%d" % i, [128, 4, 128]) for i in range(2)]
        BD = S.sb("BD", [128, 128])
        BDT = S.sb("BDT", [128, 128])
        dd = S.sb("dd", [128, 64])
        S.memset("pool", BD[:], 0.0)
        pb = S.ps("pbd", [128, 128])
        pq = S.ps("pq", [128, 64])
        for dr in range(2):
            S.dma(Tin[dr][:], D["own_st_rwc"][dr])
            order = range(8) if dr == 0 else range(7, -1, -1)
            for rr in order:
                l = tl[n % 2]
                n += 1
                S.dma(l[:], D["all_st_rw"][rr, dr])
                for q4 in range(4):
                    S.copy("dve", BD[0:64, 0:64], l[0:64, q4, 64:128])
                    S.copy("dve", BD[64:128, 64:128], l[64:128, q4, 64:128])
                    S.tr(pb[:], BD[:], C.ident[:])
                    S.act(BDT[:], pb[:], AF.Copy)
                    S.mm(pq[:], BDT[:], Tin[dr][:, q4, :])
                    S.tt("dve", dd[:], pq[:], l[:, q4, 0:64], ALU.add)
                    S.tt("dve", dd[:], dd[:], Tin[dr][:, q4, :], ALU.subtract)
                    S.stt("dve", Tin[dr][:, q4, :], dd[:], mk[:, dr, rr:rr + 1], Tin[dr][:, q4, :], ALU.mult, ALU.add)
            S.copy("dve", C.LH[dr][0:64, :, :], C.ident[0:64, 0:64].re("p (o k) -> p o k", o=1).bc([64, 8, 64]))
            for q4 in range(4):
                S.copy("dve", C.LH[dr][64:128, 2 * q4 + 1, :], Tin[dr][64:128, q4, :])
                S.dma(C.LH[dr][64:128, 2 * q4, :], Tin[dr][0:64, q4, :])


def head_norm_tok(S, src, nh, hd, eps, scr, out_fn):
    s1, s2, sq, mean, var = scr
    S.op("dve", lambda e: e.tensor_reduce(out=s1.t[:, :nh], in_=src.ap, axis=AX.X, op=ALU.add), reads=[src], writes=[s1])
    S.tt("pool", sq[:, :nh, :hd], src, src, ALU.mult)
    S.op("dve", lambda e: e.tensor_reduce(out=s2.t[:, :nh], in_=sq.t[:, :nh, :hd], axis=AX.X, op=ALU.add), reads=[sq], writes=[s2])
    S.ts("dve", mean[:, :nh], s1[:, :nh], 1.0 / hd)
    S.tt("dve", var[:, :nh], mean[:, :nh], mean[:, :nh], ALU.mult)
    S.stt("dve", var[:, :nh], s2[:, :nh], 1.0 / hd, var[:, :nh], ALU.mult, ALU.subtract)
    S.ts("dve", var[:, :nh], var[:, :nh], eps, op0=ALU.add)
    S.act(var[:, :nh], var[:, :nh], AF.Sqrt)
    S.op("dve", lambda e: e.reciprocal(out=var.t[:, :nh], in_=var.t[:, :nh]), reads=[var], writes=[var])
    for h in range(nh):
        S.ts("dve", src[:, h, :], src[:, h, :], mean[:, h:h + 1], var[:, h:h + 1], ALU.subtract, ALU.mult)


def merge_phase2(S, D, C):
    C.lg = S.sb("lg", [128, 8])
    C.dqT = S.sb("dqT", [128, 4, 2, 128], BF16)
    gch = S.sb("gch", [128, 2, 4, 16])
    with S.phase():
        lgb = S.sb("lgb", [128, 8])
        cst = S.sb("rcst", [128, 6, 128])
        cpos = S.sb("cpos", [128, 2, 16])
        S.dma(lgb[:], D["lgb"][:])
        S.dma(cst[:], D["rcst"][:])
        S.dma(cpos[:], D["cpos"][:])
        S.act(lgb[:], lgb[:], AF.Sigmoid)
        S.act(C.lg[:], lgb[:], AF.Ln)
        for h in range(4):
            S.act(C.dqT[:, h, 0, :], cst[:, 0, :], AF.Exp, scale=C.lg[:, h:h + 1])
            S.act(C.dqT[:, h, 1, :], cst[:, 1, :], AF.Exp, scale=C.lg[:, 4 + h:5 + h])
            for dr in range(2):
                S.act(gch[:, dr, h, :], cpos[:, dr, :], AF.Exp, scale=C.lg[:, dr * 4 + h:dr * 4 + h + 1])
    compose_states(S, D, C, None)
    G5 = bcast_rows(S, C, "G5", 5, 0, 1.0)
    win3 = D["w_in"].t.rearrange("(kc p) c -> p kc c", p=128)
    with S.phase():
        wgr = S.sb("wgr", [128, 8, 512], BF16)
        wgt = S.sb("wgt", [128, 8, 2048], BF16)
        wro = S.sb("wro", [128, 4, 1024], BF16)
        wwo = S.sb("wwo", [128, 4, 1024], BF16)
        wout = S.sb("wout", [128, 8, 1024], BF16)
        lnwb = S.sb("lnwb", [128, 2, 512])
        S.dma(lnwb[:], D["lnwb"][:])
        with S.phase():
            stg = [S.sb("mstg%d" % i, [128, 8, 128]) for i in range(2)]
            load_w_bf(S, D["w_in"], win3[:, :, GC:GC + 512], 512, wgr, stg)
            load_w_bf(S, D["w_in"], win3[:, :, GRET:GRET + 2048], 2048, wgt, stg)
            load_w_bf(S, D["w_ret_o"], D["w_ret_o"].t.rearrange("(kc p) c -> p kc c", p=128), 1024, wro, stg)
            load_w_bf(S, D["w_rwkv_o"], D["w_rwkv_o"].t.rearrange("(kc p) c -> p kc c", p=128), 1024, wwo, stg)
            load_w_bf(S, D["w_out"], D["w_out"].t.rearrange("(kc p) c -> p kc c", p=128), 1024, wout, stg)
        ut = [S.sb("ut%d" % i, [128, 8, 128], BF16) for i in range(2)]
        qt = [S.sb("qt%d" % i, [128, 4, 128], BF16) for i in range(2)]
        qp = S.sb("qp", [128, 4, 2, 128], BF16)
        yl = [S.sb("yl%d" % i, [128, 4, 128]) for i in range(2)]
        ye = [S.sb("ye%d" % i, [128, 8, 64]) for i in range(4)]
        bn = [S.sb("bn%d" % i, [128, 4, 128]) for i in range(2)]
        gg = [S.sb("gg%d" % i, [128, 4, 128]) for i in range(2)]
        hh = [S.sb("hh%d" % i, [128, 1024]) for i in range(2)]
        scr = (S.sb("s1", [128, 8]), S.sb("s2", [128, 8]), S.sb("sq", [128, 8, 128]), S.sb("mean", [128, 8]), S.sb("var", [128, 8]))
        yrw = S.sb("yrw", [128, 8, 64])
        yT = S.sb("yT", [64, 8, 128])
        sgr = S.sb("sgr", [128, 512])
        sgt = S.sb("sgt", [128, 2048], BF16)
        ro = S.sb("ro", [128, 512], BF16)
        roT = S.sb("roT", [128, 4, 128], BF16)
        bnt = S.sb("bnt", [128, 512])
        ggt = S.sb("ggt", [128, 512])
        msum = S.sb("msum", [128, 1024])
        msb = S.sb("msb", [128, 1024], BF16)
        msT = S.sb("msT", [128, 8, 128], BF16)
        tmpo = S.sb("tmpo", [128, 512])
        h2 = [S.sb("h2_%d" % i, [128, 1024]) for i in range(2)]
        banks = [S.ps("bk%d" % i, [128, 512]) for i in range(5)]
        ptb = S.ps("ptb", [128, 8, 128], BF16)
        prw = S.ps("prw", [64, 8, 128])
        nbk = [0]

        def bank():
            nbk[0] += 1
            return banks[nbk[0] % 5]

        for t in range(16):
            c0 = LAT0 + t * 128
            u = ut[t % 2]
            S.dma(u[:], D["ulT_sp"][:, :, c0:c0 + 128])
            q = qt[t % 2]
            S.dma(q[:], D["qT_sp"][:, :, t * 128:(t + 1) * 128])
            y = yl[t % 2]
            S.dma(y[:].re("p h e -> p (h e)"), D["yret_sp"][t * 128:(t + 1) * 128, :])
            hcur = hh[t % 2]
            S.dma(hcur[:], D["h_sp"][t * 128:(t + 1) * 128, :])
            for dr in range(2):
                S.tt("pool", qp[:, :, dr, :], q[:], C.dqT[:, :, dr, :], ALU.mult)
            for dr in range(2):
                b = bank()
                bv = b[:].re("p (h e) -> p h e", h=4)
                for h in range(4):
                    S.mm(bv[:, h, :], qp[:, h, dr, :], C.Sin_b[dr][:, h, :])
                for h in range(4):
                    S.stt("dve", y[:, h, :], bv[:, h, :], gch[:, dr, h, t:t + 1], y[:, h, :], ALU.mult, ALU.add)
            def dbg(i, view, n):
                if t == 0 and "dbg" in D:
                    dt_ = S.sb("dbgt%d" % i, [128, 1024])
                    S.copy("dve", dt_[:, :n], view)
                    S.dma(D["dbg"][i, :, :n], dt_[:, :n])
            dbg(0, y[:].re("p h e -> p (h e)"), 512)
            head_norm_tok(S, y[:], 4, 128, RET_EPS, scr, None)
            dbg(5, y[:].re("p h e -> p (h e)"), 512)
            b = bank()
            for kc in range(8):
                S.mm(b[:], u[:, kc, :], wgr[:, kc, :], start=(kc == 0), stop=(kc == 7))
            S.act(sgr[:], b[:], AF.Silu)
            S.tt("dve", ro[:], y[:].re("p h e -> p (h e)"), sgr[:], ALU.mult)
            dbg(2, ro[:], 512)
            for f in range(4):
                S.tr(ptb[:, f, :], ro[:, f * 128:(f + 1) * 128], C.identb[:])
            S.copy("dve", roT[:], ptb[:, 0:4, :])
            for g4 in range(4):
                b = bank()
                for kc in range(8):
                    S.mm(b[:], u[:, kc, :], wgt[:, kc, g4 * 512:(g4 + 1) * 512], start=(kc == 0), stop=(kc == 7))
                S.act(sgt[:, g4 * 512:(g4 + 1) * 512], b[:], AF.Sigmoid)
            for hf in range(2):
                b = bank()
                for f in range(4):
                    S.mm(b[:], roT[:, f, :], wro[:, f, hf * 512:(hf + 1) * 512], start=(f == 0), stop=(f == 3))
                S.tt("dve", msum[:, hf * 512:(hf + 1) * 512], b[:], sgt[:, hf * 512:(hf + 1) * 512], ALU.mult)
            for ch in range(2):
                for dr in range(2):
                    S.dma(ye[2 * ch + dr][:].re("p h t -> p (h t)"), D["yext_sp"][dr, 2 * t + ch])
                for h in range(8):
                    for dr in range(2):
                        S.mm(prw[:, h, ch * 64:(ch + 1) * 64], C.LH[dr][:, h, :], ye[2 * ch + dr][:, h, :], start=(dr == 0), stop=(dr == 1))
            S.act(yT[:, 0:4, :], prw[:, 0:4, :], AF.Copy)
            S.copy("dve", yT[:, 4:8, :], prw[:, 4:8, :])
            b = bank()
            bv = b[:].re("p (h v) -> p h v", h=8)
            for h in range(8):
                S.tr(bv[:, h, :], yT[:, h, :], C.ident[0:64, 0:64])
            S.act(yrw[:], bv, AF.Copy)
            dbg(1, yrw[:].re("p h v -> p (h v)"), 512)
            head_norm_tok(S, yrw[:], 8, 64, RW_EPS, scr, None)
            bt_, gt_ = bn[t % 2], gg[t % 2]
            S.dma(bt_[:], D["bonusT_sp"][:, :, t * 128:(t + 1) * 128])
            S.dma(gt_[:], D["gT_sp"][:, :, t * 128:(t + 1) * 128])
            for src, dst in ((bt_, bnt), (gt_, ggt)):
                b = bank()
                for q4 in range(4):
                    S.tr(b[:, q4 * 128:(q4 + 1) * 128], src[:, q4, :], C.ident[:])
                S.act(dst[:], b[:], AF.Copy)
            yf = yrw[:].re("p h v -> p (h v)")
            S.tt("dve", yf, yf, lnwb[:, 0, :], ALU.mult)
            S.tt("pool", yf, yf, lnwb[:, 1, :], ALU.add)
            S.tt("dve", yf, yf, bnt[:], ALU.add)
            S.tt("dve", ro[:], yf, ggt[:], ALU.mult)
            dbg(3, ro[:], 512)
            for f in range(4):
                S.tr(ptb[:, f, :], ro[:, f * 128:(f + 1) * 128], C.identb[:])
            S.copy("dve", roT[:], ptb[:, 0:4, :])
            for hf in range(2):
                b = bank()
                for f in range(4):
                    S.mm(b[:], roT[:, f, :], wwo[:, f, hf * 512:(hf + 1) * 512], start=(f == 0), stop=(f == 3))
                S.tt("dve", tmpo[:], b[:], sgt[:, 1024 + hf * 512:1024 + (hf + 1) * 512], ALU.mult)
                S.tt("pool", msum[:, hf * 512:(hf + 1) * 512], msum[:, hf * 512:(hf + 1) * 512], tmpo[:], ALU.add)
            dbg(4, msum[:], 1024)
            S.act(msb[:], msum[:], AF.Copy)
            for f in range(8):
                S.tr(ptb[:, f, :], msb[:, f * 128:(f + 1) * 128], C.identb[:])
            S.copy("dve", msT[:], ptb[:])
            hn = h2[t % 2]
            for hf in range(2):
                b = bank()
                for f in range(8):
                    S.mm(b[:], msT[:, f, :], wout[:, f, hf * 512:(hf + 1) * 512], start=(f == 0), stop=(f == 7))
                S.tt("dve", tmpo[:], b[:], G5[:, hf * 512:(hf + 1) * 512], ALU.mult)
                S.tt("pool", hn[:, hf * 512:(hf + 1) * 512], tmpo[:], hcur[:, hf * 512:(hf + 1) * 512], ALU.add)
            S.dma(D["h2_sp"][t * 128:(t + 1) * 128, :], hn[:])


def mod_reload(S, D, C):
    C.ident = S.sb("ident", [128, 128])
    C.identb = S.sb("identb", [128, 128], BF16)
    S.dma(C.ident[:], D["ident"][:])
    S.copy("dve", C.identb[:], C.ident[:])
    C.modT = S.sb("modT", [128, 72, 2])
    S.dma(C.modT[:], D["modT_sp"][:])
    C.gv = S.sb("gv", [128, 3, 8])
    S.dma(C.gv[:], D["gvec"][:])
    t = S.sb("gs2", [128, 8, 2])
    for col in range(2):
        S.stt("dve", t[:, :, col], C.modT[:, 56:64, col], 1.0, C.gv[:, 2, :], ALU.add, ALU.mult)
    C.gs2 = t


def ffn_phase2(S, D, C):
    G8 = bcast_rows(S, C, "G8", 8, 0, 0.5)
    tiles = [dict(rows=128, hrow=i * 128) for i in range(16)]
    sbs = [tiles[0:6], tiles[6:12], tiles[12:16]]
    with S.phase():
        gfin = S.sb("gfin", [128, 1024])
        S.dma(gfin[:], D["gfin"][:])
        wd = S.sb("wd", [128, NFC, 1024], BF16)
        stg = [S.sb("stg%d" % i, [128, 8, 128]) for i in range(4)]
        wgb = [S.sb("wgb%d" % i, [128, 8, 128], BF16) for i in range(4)]
        wdsrc = D["f2_wd"].t.rearrange("(fc p) d -> p fc d", p=128)
        wdst = [S.sb("wdst%d" % i, [128, 1024]) for i in range(2)]
        for fc in range(NFC):
            S.dma(wdst[fc % 2][:], View(D["f2_wd"], wdsrc[:, fc, :]))
            S.copy("pool", wd[:, fc, :], wdst[fc % 2][:])
        u1 = S.sb("u1", [128, 8, 768], BF16)
        actT = S.sb("actT", [128, NFC, 768], BF16)
        xt = [S.sb("xt%d" % i, [128, 1024]) for i in range(2)]
        ht = [S.sb("ht%d" % i, [128, 1024]) for i in range(2)]
        ot = [S.sb("ot%d" % i, [128, 1024]) for i in range(2)]
        scr = (S.sb("ss", [128, 1]), S.sb("rs", [128, 1]), S.sb("junk", [128, 1024], BF16), S.sb("xn", [128, 1024], BF16))
        sg = [S.sb("sg%d" % i, [128, 512]) for i in range(2)]
        tmpd = [S.sb("tmpd%d" % i, [128, 512]) for i in range(2)]
        pst = S.ps("pst", [128, 8, 128], BF16)
        psg = [S.ps("psg%d" % i, [128, 512]) for i in range(2)]
        psu = [S.ps("psu%d" % i, [128, 512]) for i in range(2)]
        psd = [S.ps("psd%d" % i, [128, 512]) for i in range(2)]
        wgsrc = D["f2_wg"].t.rearrange("(kc p) f -> p kc f", p=128)
        wusrc = D["f2_wu"].t.rearrange("(kc p) f -> p kc f", p=128)
        nload = 0
        for sbi, sb in enumerate(sbs):
            col = 0
            for ti, t in enumerate(sb):
                x = xt[ti % 2]
                S.dma(x[:], D["h2_sp"][t["hrow"]:t["hrow"] + 128, :])
                c0 = col
                norm_to_T(S, C, x, 128, C.gs2, 6, 0, pst, lambda dc, c0=c0: u1[:, dc, c0:c0 + 128], scr)
                t["c0"] = c0
                col += 128
            ncol = col
            groups = [(g0, min(512, ncol - g0)) for g0 in range(0, ncol, 512)]
            it = 0
            for fc in range(NFC):
                k = nload % 2
                nload += 1
                S.dma(stg[2 * k][:], View(D["f2_wg"], wgsrc[:, :, fc * 128:(fc + 1) * 128]))
                S.dma(stg[2 * k + 1][:], View(D["f2_wu"], wusrc[:, :, fc * 128:(fc + 1) * 128]))
                S.copy("pool", wgb[2 * k][:], stg[2 * k][:])
                S.copy("pool", wgb[2 * k + 1][:], stg[2 * k + 1][:])
                for (g0, gn) in groups:
                    pg, pu = psg[it % 2], psu[it % 2]
                    for kc in range(8):
                        S.mm(pg[:, :gn], wgb[2 * k][:, kc, :], u1[:, kc, g0:g0 + gn], start=(kc == 0), stop=(kc == 7))
                    for kc in range(8):
                        S.mm(pu[:, :gn], wgb[2 * k + 1][:, kc, :], u1[:, kc, g0:g0 + gn], start=(kc == 0), stop=(kc == 7))
                    s = sg[it % 2]
                    S.act(s[:, :gn], pg[:, :gn], AF.Silu)
                    S.tt("dve", actT[:, fc, g0:g0 + gn], s[:, :gn], pu[:, :gn], ALU.mult)
                    it += 1
            for ti, t in enumerate(sb):
                c0 = t["c0"]
                x = xt[ti % 2]
                h = ht[ti % 2]
                o = ot[ti % 2]
                S.dma(x[:], D["h2_sp"][t["hrow"]:t["hrow"] + 128, :])
                for hf in range(2):
                    for fc in range(NFC):
                        S.mm(psd[hf][:, :], actT[:, fc, c0:c0 + 128], wd[:, fc, hf * 512:(hf + 1) * 512],
                             start=(fc == 0), stop=(fc == NFC - 1))
                for hf in range(2):
                    S.tt("dve", tmpd[hf][:], psd[hf][:], G8[:, hf * 512:(hf + 1) * 512], ALU.mult)
                    S.tt("pool", h[:, hf * 512:(hf + 1) * 512], tmpd[hf][:], x[:, hf * 512:(hf + 1) * 512], ALU.add)
                ss, rs, junk, xn = scr
                S.memset("pool", ss[:], 0.0)
                S.act(junk[:], h[:], AF.Square, accum_out=ss[:])
                S.ts("dve", rs[:], ss[:], 1.0 / 1024, EPS, ALU.mult, ALU.add)
                S.act(rs[:], rs[:], AF.Sqrt)
                S.op("dve", lambda e: e.reciprocal(out=rs.t[:], in_=rs.t[:]), reads=[rs], writes=[rs])
                S.act(o[:], h[:], AF.Copy, scale=rs[:, 0:1])
                S.tt("pool", o[:], o[:], gfin[:], ALU.mult)
                S.dma(D["out"][t["hrow"]:t["hrow"] + 128, :], o[:])


SP = {"h_sp": ([2048, 1024], F32), "qT_sp": ([128, 4, 2048], BF16), "yret_sp": ([2048, 512], F32),
      "ulT_sp": ([128, 8, UC], BF16), "yext_sp": ([2, 32, 128, 512], F32), "bonusT_sp": ([128, 4, 2048], F32),
      "gT_sp": ([128, 4, 2048], F32), "modT_sp": ([128, 72, 2], F32)}
ST = {"st_ret": ([2, 2, 128, 4, 128], F32), "st_rwc": ([2, 128, 4, 64], F32), "st_rw": ([2, 128, 4, 128], F32)}
def build1():
    nc = bass.Bass("TRN2", target_bir_lowering=False)
    S = Sched(nc)
    D = {}
    for n, shp in IN_SHAPES.items():
        D[n] = S.dram(n, nc.dram_tensor(n, shp, F32, kind="ExternalInput").ap())
    for n, (shp, dt) in {**SP, **ST}.items():
        D[n] = S.dram(n, nc.dram_tensor(n, shp, dt, kind="ExternalOutput").ap())
    C = Ctx()
    mod_phase(S, D, C)
    S.dma(D["modT_sp"][:], C.modT[:])
    ffn_phase1(S, D, C)
    retention_phase1(S, D, C)
    rwkv_phase1(S, D, C)
    S.finish()
    return nc
P2_IN = ["ident", "gvec", "lgb", "rcst", "w_in"]
def build2():
    nc = bass.Bass("TRN2", target_bir_lowering=False)
    S = Sched(nc)
    D = {}
    for n in P2_IN:
        D[n] = S.dram(n, nc.dram_tensor(n, IN_SHAPES[n], F32, kind="ExternalInput").ap())
    for n, shp in P2_SHAPES.items():
        D[n] = S.dram(n, nc.dram_tensor(n, shp, F32, kind="ExternalInput").ap())
    for n, (shp, dt) in SP.items():
        D[n] = S.dram(n, nc.dram_tensor(n, shp, dt, kind="ExternalInput").ap())
    D["own_st_ret"] = S.dram("own_st_ret", nc.dram_tensor("own_st_ret", ST["st_ret"][0], F32, kind="ExternalInput").ap())
    D["own_st_rwc"] = S.dram("own_st_rwc", nc.dram_tensor("own_st_rwc", ST["st_rwc"][0], F32, kind="ExternalInput").ap())
    D["all_st_ret"] = S.dram("all_st_ret", nc.dram_tensor("all_st_ret", [8] + ST["st_ret"][0], F32, kind="ExternalInput").ap())
    D["all_st_rw"] = S.dram("all_st_rw", nc.dram_tensor("all_st_rw", [8] + ST["st_rw"][0], F32, kind="ExternalInput").ap())
    D["h2_sp"] = S.dram("h2_sp", nc.dram_tensor("h2_sp", [2048, 1024], F32, kind="ExternalOutput").ap())
    D["out"] = S.dram("out", nc.dram_tensor("out", [2048, 1024], F32, kind="ExternalOutput").ap())
    D["dbg"] = S.dram("dbg", nc.dram_tensor("dbg", [8, 128, 1024], F32, kind="ExternalOutput").ap())
    C = Ctx()
    mod_reload(S, D, C)
    merge_phase2(S, D, C)
    ffn_phase2(S, D, C)
    S.finish()
    return nc


def build_fused():
    nc = bass.Bass("TRN2", target_bir_lowering=False)
    S = Sched(nc)
    D = {}
    for n, shp in {**IN_SHAPES, **P2_SHAPES}.items():
        D[n] = S.dram(n, nc.dram_tensor(n, shp, F32, kind="ExternalInput").ap())
    for n, (shp, dt) in {**SP, **ST}.items():
        D[n] = S.dram(n, nc.dram_tensor(n, shp, dt).ap())
    D["h2_sp"] = S.dram("h2_sp", nc.dram_tensor("h2_sp", [2048, 1024], F32).ap())
    D["out"] = S.dram("out", nc.dram_tensor("out", [2048, 1024], F32, kind="ExternalOutput").ap())
    snd = nc.dram_tensor("st_snd", [512, 512], F32)
    rcv = nc.dram_tensor("st_rcv", [4096, 512], F32)
    dsnd, drcv = S.dram("st_snd", snd.ap()), S.dram("st_rcv", rcv.ap())
    C = Ctx()
    mod_phase(S, D, C)
    ffn_phase1(S, D, C)
    retention_phase1(S, D, C)
    rwkv_phase1(S, D, C)
    sv = snd.ap().rearrange("(x p) (h e) -> x p h e", p=128, h=4)
    for dr in range(2):
        S.dma(View(dsnd, sv[dr]), D["st_ret"][dr, 0])
        S.dma(View(dsnd, sv[2 + dr]), D["st_rw"][dr])
    S.collective(drcv, dsnd, snd.ap().opt(), rcv.ap().opt())
    rv = rcv.ap().rearrange("(r x o p) (h e) -> r x o p h e", r=8, x=4, o=1, p=128, h=4)
    D["all_st_ret"] = S.dram("all_st_ret", rv)
    D["all_st_rw"] = S.dram("all_st_rw", rv[:, 2:4, 0])
    for n in ("all_st_ret", "all_st_rw"):
        D[n].last_w = drcv.last_w
    D["own_st_ret"] = D["st_ret"]
    D["own_st_rwc"] = D["st_rwc"]
    merge_phase2(S, D, C)
    ffn_phase2(S, D, C)
    S.finish()
    return nc


_CACHE = {}


def kernel(**inp):
    inp = {k: np.asarray(v) for k, v in inp.items()}
    if "nc" not in _CACHE:
        _CACHE["nc"] = build_fused()
    nc = _CACHE["nc"]
    com = {**prep_common(inp), **prep_ret_common(inp), **prep_rw_common(inp), **prep_p2_common(inp)}
    maps = [{**com, **prep_core(inp, r), **prep_ret_core(inp, r), **prep_p2_core(r)} for r in range(8)]
    res = run_bass_kernel_spmd(nc, maps, core_ids=list(range(8))).results
    out = np.stack([r_["out"] for r_ in res]).reshape(2, 8192, 1024)
    return np.ascontiguousarray(out.astype(np.float32))
```

```python
import os
import contextlib
import numpy as np
import ml_dtypes
import concourse.bass as bass
import concourse.mybir as mybir
from concourse.bass_utils import run_bass_kernel_spmd

F32 = mybir.dt.float32
BF16 = mybir.dt.bfloat16
AF = mybir.ActivationFunctionType
ALU = mybir.AluOpType
AX = mybir.AxisListType
NDSEM = 92


class Res:
    def __init__(self, name, t=None, kind="sb"):
        self.name = name
        self.t = t
        self.kind = kind
        self.last_w = None
        self.reads = {}
        self.dsem = None

    def __getitem__(self, idx):
        return View(self, self.t[idx])


class View:
    def __init__(self, res, ap):
        self.res = res
        self.ap = ap

    def __getitem__(self, idx):
        return View(self.res, self.ap[idx])

    def bc(self, shape):
        return View(self.res, self.ap.to_broadcast(list(shape)))

    def re(self, pat, **kw):
        return View(self.res, self.ap.rearrange(pat, **kw))


def _ap(x):
    return x.ap if isinstance(x, View) else x


class Sched:
    ENG = ("pe", "act", "dve", "pool", "sp")

    def __init__(self, nc):
        self.nc = nc
        self.cnt = {e: 0 for e in self.ENG}
        self.waited = {e: {} for e in self.ENG}
        self.sems = {}
        self.dma_cnt = {}
        self.n_dsem = 0
        self.eobj = {"pe": nc.tensor, "act": nc.scalar, "dve": nc.vector, "pool": nc.gpsimd, "sp": nc.sync}
        self.sem_cms = []
        for e in self.ENG:
            if e != "sp":
                cm = nc.semaphore("s_eng_" + e)
                self.sems[("eng", e)] = cm.__enter__()
                self.sem_cms.append(cm)
        for i in range(NDSEM):
            cm = nc.semaphore("s_dma_%d" % i)
            self.sems[("dma", i)] = cm.__enter__()
            self.sem_cms.append(cm)
            self.dma_cnt[("dma", i)] = 0
        self.stack = contextlib.ExitStack()
        self.nalloc = 0
        self.nins = {e: 0 for e in self.ENG}
        self.incpts = {e: [] for e in self.ENG}
        self.last_h = {e: None for e in self.ENG}

    def sb(self, name, shape, dt=F32, stack=None):
        self.nalloc += 1
        t = (stack or self.stack).enter_context(self.nc.sbuf_tensor("%s_%d" % (name, self.nalloc), list(shape), dt))
        return Res(name, t)

    def ps(self, name, shape, dt=F32, stack=None):
        self.nalloc += 1
        t = (stack or self.stack).enter_context(self.nc.psum_tensor("%s_%d" % (name, self.nalloc), list(shape), dt))
        return Res(name, t, "ps")

    def dram(self, name, ap):
        return Res(name, ap, "dram")

    @contextlib.contextmanager
    def phase(self):
        old = self.stack
        with contextlib.ExitStack() as st:
            self.stack = st
            try:
                yield st
            finally:
                self.barrier()
                self.stack = old

    def _deps(self, eng, reads, writes):
        deps = {}

        def add(ev):
            if ev is None:
                return
            k, v = ev
            if eng == "pe" and k == ("eng", "pe"):
                return
            if k[0] == "dma":
                v = self.dma_cnt[k]
            if deps.get(k, 0) < v:
                deps[k] = v
        for r in reads:
            add(r.last_w)
        for w in writes:
            add(w.last_w)
            for k, v in w.reads.items():
                add((k, v))
        out = []
        wd = self.waited[eng]
        for k, v in deps.items():
            if k[0] == "eng":
                v = self._resolve(k[1], v)
            if wd.get(k, 0) < v:
                wd[k] = v
                out.append((k, v))
        return out

    def _resolve(self, e, idx):
        pts = self.incpts[e]
        lo, hi = 0, len(pts)
        while lo < hi:
            mid = (lo + hi) // 2
            if pts[mid][0] >= idx:
                hi = mid
            else:
                lo = mid + 1
        if lo < len(pts):
            return pts[lo][1]
        cnt = len(pts) + 1
        self.last_h[e].then_inc(self.sems[("eng", e)], 1)
        pts.append((self.nins[e], cnt))
        return cnt

    def _record(self, ev, reads, writes):
        k, v = ev
        for r in reads:
            if r.reads.get(k, 0) < v:
                r.reads[k] = v
        for w in writes:
            w.last_w = ev
            w.reads = {}

    def op(self, eng, fn, reads=(), writes=()):
        reads = [r.res if isinstance(r, View) else r for r in reads if r is not None and not isinstance(r, (int, float))]
        writes = [w.res if isinstance(w, View) else w for w in writes if w is not None]
        waits = self._deps(eng, reads, writes)
        self.cnt[eng] += 1
        self.nins[eng] += 1
        ev = (("eng", eng), self.nins[eng])
        eo = self.eobj[eng]
        for k, v in waits:
            eo.wait_ge(self.sems[k], v)
        self.last_h[eng] = fn(eo)
        self._record(ev, reads, writes)

    def dma(self, out, in_, queue="sp", **kw):
        ro, ri = out.res, in_.res
        owner = ri if ro.kind == "dram" else ro
        if owner.dsem is None:
            owner.dsem = ("dma", self.n_dsem % NDSEM)
            self.n_dsem += 1
        waits = self._deps(queue, [ri], [ro])
        self.dma_cnt[owner.dsem] += 16
        ev = (owner.dsem, self.dma_cnt[owner.dsem])
        eo = self.eobj[queue]
        for k, v in waits:
            eo.wait_ge(self.sems[k], v)
        eo.dma_start(out=out.ap, in_=in_.ap, **kw).then_inc(self.sems[owner.dsem], 16)
        self._record(ev, [ri], [ro])

    def collective(self, out_res, in_res, in_ap, out_ap, kind="AllGather", ncores=8):
        if out_res.dsem is None:
            out_res.dsem = ("dma", self.n_dsem % NDSEM)
            self.n_dsem += 1
        key = out_res.dsem
        waits = self._deps("pool", [in_res], [out_res])
        eo = self.eobj["pool"]
        for k, v in waits:
            eo.wait_ge(self.sems[k], v)
        self.dma_cnt[key] += 1
        eo.collective_compute(kind, ALU.bypass, replica_groups=[list(range(ncores))], ins=[in_ap], outs=[out_ap]).then_inc(self.sems[key], 1)
        self._record((key, self.dma_cnt[key]), [in_res], [out_res])

    def barrier(self):
        evs = [(("eng", g), self._resolve(g, self.nins[g])) for g in self.ENG if g != "sp" and self.nins[g] > 0]
        evs += [(k, v) for k, v in self.dma_cnt.items() if v > 0]
        for e in self.ENG:
            wd = self.waited[e]
            for k, v in evs:
                if wd.get(k, 0) < v:
                    wd[k] = v
                    self.eobj[e].wait_ge(self.sems[k], v)

    def finish(self):
        self.barrier()
        self.stack.close()
        for cm in reversed(self.sem_cms):
            cm.__exit__(None, None, None)

    def mm(self, out, lhsT, rhs, start=True, stop=True, **kw):
        self.op("pe", lambda e: e.matmul(out.ap, lhsT=lhsT.ap, rhs=rhs.ap, start=start, stop=stop, **kw),
                reads=[lhsT, rhs], writes=[out])

    def tr(self, out, in_, ident):
        self.op("pe", lambda e: e.transpose(out=out.ap, in_=in_.ap, identity=ident.ap), reads=[in_, ident], writes=[out])

    def act(self, out, in_, func, scale=1.0, bias=0.0, accum_out=None, eng="act"):
        kw = {}
        if accum_out is not None:
            kw["accum_out"] = accum_out.ap
        self.op(eng, lambda e: e.activation(out=out.ap, in_=in_.ap, func=func, scale=_ap(scale), bias=_ap(bias), **kw),
                reads=[in_, scale, bias], writes=[out, accum_out])

    def tt(self, eng, out, in0, in1, op):
        self.op(eng, lambda e: e.tensor_tensor(out=out.ap, in0=in0.ap, in1=in1.ap, op=op), reads=[in0, in1], writes=[out])

    def ts(self, eng, out, in0, s1, s2=None, op0=ALU.mult, op1=None):
        if op1 is None:
            self.op(eng, lambda e: e.tensor_scalar(out=out.ap, in0=in0.ap, scalar1=_ap(s1), scalar2=None, op0=op0),
                    reads=[in0, s1], writes=[out])
        else:
            self.op(eng, lambda e: e.tensor_scalar(out=out.ap, in0=in0.ap, scalar1=_ap(s1), scalar2=_ap(s2), op0=op0, op1=op1),
                    reads=[in0, s1, s2], writes=[out])

    def stt(self, eng, out, in0, scalar, in1, op0, op1):
        self.op(eng, lambda e: e.scalar_tensor_tensor(out=out.ap, in0=in0.ap, scalar=_ap(scalar), in1=in1.ap, op0=op0, op1=op1),
                reads=[in0, scalar, in1], writes=[out])

    def copy(self, eng, out, in_):
        if eng == "act":
            self.op(eng, lambda e: e.copy(out=out.ap, in_=in_.ap), reads=[in_], writes=[out])
        else:
            self.op(eng, lambda e: e.tensor_copy(out=out.ap, in_=in_.ap), reads=[in_], writes=[out])

    def memset(self, eng, out, val):
        self.op(eng, lambda e: e.memset(out.ap, val), writes=[out])


D_FF = 2816
NFC = 22
EPS = 1e-6
UC = 2308
LAT0 = 1
CTX0 = 2051


def fm(v):
    return np.ascontiguousarray(np.asarray(v, np.float32).reshape(-1, 128).T)


def prep_common(inp):
    d = {}
    d["ident"] = np.eye(128, dtype=np.float32)
    d["w_mod"] = inp["w_mod"][0]
    d["bmodT"] = fm(inp["b_mod"][0])
    d["gvec"] = np.ascontiguousarray(np.stack([fm(inp["g_ffn1"][0]), fm(inp["g_mix"][0]), fm(inp["g_ffn2"][0])], 1))
    d["f1_wg"] = inp["ffn1_w_gate"][0]
    d["f1_wu"] = inp["ffn1_w_up"][0]
    d["f1_wd"] = inp["ffn1_w_down"][0]
    return d


def prep_core(inp, r):
    b, j = r // 4, r % 4
    x = inp["x"][b]
    lo = j * 2048
    d = {}
    d["x_lat"] = np.ascontiguousarray(x[lo:lo + 2048])
    halo = np.zeros((2, 1024), np.float32)
    hm = np.zeros((128, 2), np.float32)
    if j > 0:
        halo[0] = x[lo - 1]
        hm[:, 0] = 1
    if j < 3:
        halo[1] = x[lo + 2048]
        hm[:, 1] = 1
    d["x_halo"] = halo
    d["halo_mask"] = hm
    d["x_ctx"] = np.ascontiguousarray(inp["ctx"][b])
    d["cT"] = np.ascontiguousarray(np.stack([fm(inp["c"][b]), fm(inp["c_ctx"])], -1))
    return d


IN_SHAPES = {
    "ident": [128, 128], "w_mod": [1024, 9216], "bmodT": [128, 72], "gvec": [128, 3, 8],
    "f1_wg": [1024, D_FF], "f1_wu": [1024, D_FF], "f1_wd": [D_FF, 1024],
    "x_lat": [2048, 1024], "x_halo": [2, 1024], "halo_mask": [128, 2], "x_ctx": [256, 1024], "cT": [128, 8, 2],
}


class Ctx:
    pass


def mod_phase(S, D, C):
    C.ident = S.sb("ident", [128, 128])
    C.identb = S.sb("identb", [128, 128], BF16)
    S.dma(C.ident[:], D["ident"][:])
    S.copy("dve", C.identb[:], C.ident[:])
    C.modT = S.sb("modT", [128, 72, 2])
    C.gv = S.sb("gv", [128, 3, 8])
    S.dma(C.gv[:], D["gvec"][:])
    with S.phase():
        cT = S.sb("cT", [128, 8, 2])
        sc = S.sb("sc", [128, 8, 2])
        bm = S.sb("bm", [128, 72])
        S.dma(cT[:], D["cT"][:])
        S.dma(bm[:], D["bmodT"][:])
        S.act(sc[:], cT[:], AF.Silu)
        psm = S.ps("psm", [128, 144])
        wb = [S.sb("wm%d" % i, [128, 8, 1024]) for i in range(2)]
        wsrc = D["w_mod"].t.rearrange("(kc p) c -> p kc c", p=128)
        for m in range(9):
            w = wb[m % 2]
            for kc in range(8):
                S.dma(w[:, kc, :], View(D["w_mod"], wsrc[:, kc, m * 1024:(m + 1) * 1024]))
            for oc in range(8):
                o = m * 8 + oc
                for kc in range(8):
                    S.mm(psm[:, 2 * o:2 * o + 2], w[:, kc, oc * 128:(oc + 1) * 128], sc[:, kc, :],
                         start=(kc == 0), stop=(kc == 7))
        psv = psm[:].re("p (o c) -> p o c", c=2)
        for col in range(2):
            S.tt("dve", C.modT[:, :, col], psv[:, :, col], bm[:], ALU.add)
    def mk_gs(name, gi, scale_idx):
        t = S.sb(name, [128, 8, 2])
        for col in range(2):
            S.stt("dve", t[:, :, col], C.modT[:, scale_idx * 8:(scale_idx + 1) * 8, col], 1.0, C.gv[:, gi, :], ALU.add, ALU.mult)
        return t
    C.gs1 = mk_gs("gs1", 0, 1)
    C.gsm = mk_gs("gsm", 1, 4)
    C.gs2 = mk_gs("gs2", 2, 7)


def bcast_rows(S, C, name, idx, col, mul):
    G = S.sb(name, [128, 1024])
    with S.phase():
        tmp = S.sb("bct", [128, 128])
        psb = [S.ps("psb%d" % i, [128, 512]) for i in range(2)]
        for dc in range(8):
            S.copy("dve", tmp[:], C.modT[:, idx * 8 + dc, col:col + 1].bc([128, 128]))
            S.mm(psb[dc // 4][:, (dc % 4) * 128:(dc % 4 + 1) * 128], tmp[:], C.ident[:])
        for hf in range(2):
            S.act(G[:, hf * 512:(hf + 1) * 512], psb[hf][:], AF.Copy, scale=mul)
    return G


def norm_to_T(S, C, xt, rows, gs, sh_idx, col, pst, dst_fn, scr):
    ss, rs, junk, xn = scr
    S.memset("pool", ss[:rows, :], 0.0)
    S.act(junk[:rows, :], xt[:rows, :], AF.Square, accum_out=ss[:rows, :])
    S.ts("dve", rs[:rows, :], ss[:rows, :], 1.0 / 1024, EPS, ALU.mult, ALU.add)
    S.act(rs[:rows, :], rs[:rows, :], AF.Sqrt)
    S.op("dve", lambda e: e.reciprocal(out=rs.t[:rows, :], in_=rs.t[:rows, :]), reads=[rs], writes=[rs])
    S.act(xn[:rows, :], xt[:rows, :], AF.Copy, scale=rs[:rows, 0:1])
    for dc in range(8):
        S.tr(pst[:, dc, :rows], xn[:rows, dc * 128:(dc + 1) * 128], C.identb[:rows, :rows])
    for dc in range(8):
        o = dst_fn(dc)
        if dc % 2 == 0:
            S.ts("dve", o, pst[:, dc, :rows], gs[:, dc, col:col + 1], C.modT[:, sh_idx * 8 + dc, col:col + 1], ALU.mult, ALU.add)
        else:
            S.act(o, pst[:, dc, :rows], AF.Identity, scale=gs[:, dc, col:col + 1], bias=C.modT[:, sh_idx * 8 + dc, col:col + 1])


def ffn_phase1(S, D, C):
    with S.phase():
        _ffn_phase1(S, D, C)
        S.dma(D["ulT_sp"][:], C.ulT[:])


def _ffn_phase1(S, D, C):
    C.ulT = S.sb("ulT", [128, 8, UC], BF16)
    S.memset("pool", C.ulT[:, :, 2050:2051], 0.0)
    S.memset("pool", C.ulT[:, :, 2307:2308], 0.0)
    G1 = [bcast_rows(S, C, "G1l", 2, 0, 0.5), bcast_rows(S, C, "G1c", 2, 1, 0.5)]
    tiles = []
    for i in range(16):
        tiles.append(dict(src=("x_lat", i * 128), rows=128, mc=0, ucol=LAT0 + i * 128, hrow=i * 128))
    for i in range(2):
        tiles.append(dict(src=("x_ctx", i * 128), rows=128, mc=1, ucol=CTX0 + i * 128, hrow=None))
    tiles.append(dict(src=("x_halo", 0), rows=2, mc=0, ucol=None, hrow=None))
    sbs = [tiles[0:6], tiles[6:12], tiles[12:19]]
    with S.phase():
        wd = S.sb("wd", [128, NFC, 1024], BF16)
        stg = [S.sb("stg%d" % i, [128, 8, 128]) for i in range(4)]
        wgb = [S.sb("wgb%d" % i, [128, 8, 128], BF16) for i in range(4)]
        wdsrc = D["f1_wd"].t.rearrange("(fc p) d -> p fc d", p=128)
        wdst = [S.sb("wdst%d" % i, [128, 1024]) for i in range(2)]
        for fc in range(NFC):
            S.dma(wdst[fc % 2][:], View(D["f1_wd"], wdsrc[:, fc, :]))
            S.copy("pool", wd[:, fc, :], wdst[fc % 2][:])
        u1 = S.sb("u1", [128, 8, 770], BF16)
        actT = S.sb("actT", [128, NFC, 770], BF16)
        xt = [S.sb("xt%d" % i, [128, 1024]) for i in range(2)]
        ht = [S.sb("ht%d" % i, [128, 1024]) for i in range(2)]
        scr = (S.sb("ss", [128, 1]), S.sb("rs", [128, 1]), S.sb("junk", [128, 1024], BF16), S.sb("xn", [128, 1024], BF16))
        sg = [S.sb("sg%d" % i, [128, 512]) for i in range(2)]
        tmpd = [S.sb("tmpd%d" % i, [128, 512]) for i in range(2)]
        pst = S.ps("pst", [128, 8, 128], BF16)
        psg = [S.ps("psg%d" % i, [128, 512]) for i in range(2)]
        psu = [S.ps("psu%d" % i, [128, 512]) for i in range(2)]
        psd = [S.ps("psd%d" % i, [128, 512]) for i in range(2)]
        wgsrc = D["f1_wg"].t.rearrange("(kc p) f -> p kc f", p=128)
        wusrc = D["f1_wu"].t.rearrange("(kc p) f -> p kc f", p=128)
        nload = [0]
        xtn = [S.sb("xtn%d" % i, [128, 1024]) for i in range(2)]
        scr2 = (S.sb("ss2", [128, 1]), S.sb("rs2", [128, 1]), S.sb("junk2", [128, 1024], BF16), S.sb("xn2", [128, 1024], BF16))
        pst2 = S.ps("pst2", [128, 8, 128], BF16)

        def norm_tile(sb, ti):
            t = sb[ti]
            x = xtn[ti % 2]
            rows = t["rows"]
            c0 = sum(t_["rows"] for t_ in sb[:ti])
            S.dma(x[:rows, :], D[t["src"][0]][t["src"][1]:t["src"][1] + rows, :])
            norm_to_T(S, C, x, rows, C.gs1, 0, t["mc"], pst2, lambda dc, c0=c0, rows=rows: u1[:, dc, c0:c0 + rows], scr2)
            t["c0"] = c0

        def upgate(sb):
            ncol = sum(t_["rows"] for t_ in sb)
            groups = [(g0, min(512, ncol - g0)) for g0 in range(0, ncol, 512)]
            it = 0
            for fc in range(NFC):
                k = nload[0] % 2
                nload[0] += 1
                S.dma(stg[2 * k][:], View(D["f1_wg"], wgsrc[:, :, fc * 128:(fc + 1) * 128]))
                S.dma(stg[2 * k + 1][:], View(D["f1_wu"], wusrc[:, :, fc * 128:(fc + 1) * 128]))
                S.copy("pool", wgb[2 * k][:], stg[2 * k][:])
                S.copy("pool", wgb[2 * k + 1][:], stg[2 * k + 1][:])
                for (g0, gn) in groups:
                    pg, pu = psg[it % 2], psu[it % 2]
                    for kc in range(8):
                        S.mm(pg[:, :gn], wgb[2 * k][:, kc, :], u1[:, kc, g0:g0 + gn], start=(kc == 0), stop=(kc == 7))
                    for kc in range(8):
                        S.mm(pu[:, :gn], wgb[2 * k + 1][:, kc, :], u1[:, kc, g0:g0 + gn], start=(kc == 0), stop=(kc == 7))
                    s = sg[it % 2]
                    S.act(s[:, :gn], pg[:, :gn], AF.Silu)
                    S.tt("dve", actT[:, fc, g0:g0 + gn], s[:, :gn], pu[:, :gn], ALU.mult)
                    it += 1
        def down_tile(sb, ti):
            if True:
                t = sb[ti]
                rows, c0 = t["rows"], t["c0"]
                x = xt[ti % 2]
                h = ht[ti % 2]
                S.dma(x[:rows, :], D[t["src"][0]][t["src"][1]:t["src"][1] + rows, :])
                for hf in range(2):
                    for fc in range(NFC):
                        S.mm(psd[hf][:rows, :], actT[:, fc, c0:c0 + rows], wd[:, fc, hf * 512:(hf + 1) * 512],
                             start=(fc == 0), stop=(fc == NFC - 1))
                for hf in range(2):
                    S.tt("dve", tmpd[hf][:rows, :], psd[hf][:rows, :], G1[t["mc"]][:rows, hf * 512:(hf + 1) * 512], ALU.mult)
                    S.tt("pool", h[:rows, hf * 512:(hf + 1) * 512], tmpd[hf][:rows, :], x[:rows, hf * 512:(hf + 1) * 512], ALU.add)
                if t["hrow"] is not None:
                    S.dma(D["h_sp"][t["hrow"]:t["hrow"] + 128, :], h[:, :])
                if t["ucol"] is not None:
                    uc = t["ucol"]
                    dst = lambda dc, uc=uc, rows=rows: C.ulT[:, dc, uc:uc + rows]
                else:
                    dst = lambda dc: C.ulT[:, dc, 0:2050:2049]
                norm_to_T(S, C, h, rows, C.gsm, 3, t["mc"], pst, dst, scr)

        for ti in range(len(sbs[0])):
            norm_tile(sbs[0], ti)
        for k_, sb in enumerate(sbs):
            upgate(sb)
            nxt = sbs[k_ + 1] if k_ + 1 < len(sbs) else []
            for ti in range(max(len(sb), len(nxt))):
                if ti < len(sb):
                    down_tile(sb, ti)
                if ti < len(nxt):
                    norm_tile(nxt, ti)
        hm = S.sb("hm", [128, 2])
        S.dma(hm[:], D["halo_mask"][:])
        S.ts("dve", C.ulT[:, :, 0], C.ulT[:, :, 0], hm[:, 0:1])
        S.ts("dve", C.ulT[:, :, 2049], C.ulT[:, :, 2049], hm[:, 1:2])


K_SCALE = 128 ** -0.5
QC, KC, VC, GC = 0, 512, 1024, 1536
RC, RKC, RVC, GRET, GRW = 2048, 2560, 3072, 3584, 4608


def rope_partner():
    d = np.arange(128)
    return np.where(d % 64 < 32, d + 32, d - 32)


def prep_ret_common(inp):
    d = {}
    w_in = inp["w_in"][0]
    d["w_in"] = w_in
    perm = rope_partner()
    cols = []
    for base in (QC, KC):
        for h in range(4):
            cols.append(base + h * 128 + perm)
    d["w_rot"] = np.ascontiguousarray(w_in[:, np.concatenate(cols)])
    d["lgb"] = np.ascontiguousarray(np.broadcast_to(inp["ret_decay_logit"][0].reshape(1, 8), (128, 8))).astype(np.float32)
    i = np.arange(128, dtype=np.float32)
    j = i[:, None]
    ii = i[None, :]
    cst = np.zeros((128, 6, 128), np.float32)
    cst[:, 0] = ii + 1
    cst[:, 1] = 128 - ii
    cst[:, 2] = np.maximum(ii - j, 0)
    cst[:, 3] = np.maximum(j - ii, 0)
    cst[:, 4] = (j <= ii)
    cst[:, 5] = (j > ii)
    d["rcst"] = cst
    d["pcol"] = np.stack([127 - i, i], -1).astype(np.float32)
    return d


def prep_ret_core(inp, r):
    j = r % 4
    t = j * 2048 + np.arange(2048)
    rows, cols = t // 64, t % 64
    inv = (10000.0 ** (-np.arange(32, dtype=np.float32) / 32)).astype(np.float32)
    dd = np.arange(128)
    pos = np.where(dd[:, None] < 64, rows[None, :], cols[None, :]).astype(np.float32)
    ang = pos * inv[dd % 32][:, None]
    sgn = np.where(dd % 64 < 32, -1.0, 1.0)[:, None]
    return {"cosT": np.cos(ang).astype(np.float32), "sinT": (np.sin(ang) * sgn).astype(np.float32)}


IN_SHAPES.update({"w_in": [1024, 5632], "w_rot": [1024, 1024], "lgb": [128, 8], "rcst": [128, 6, 128], "pcol": [128, 2],
                  "cosT": [128, 2048], "sinT": [128, 2048]})


def load_w_chunk(S, Dres, src3, c0, stg, wb):
    S.dma(stg[:], View(Dres, src3[:, :, c0:c0 + 128]))
    S.copy("pool", wb[:], stg[:])


def ret_tables(S, D, C):
    C.lg = S.sb("lg", [128, 8])
    C.gC = S.sb("gC", [128, 8])
    C.Mc = S.sb("Mc", [128, 4, 128], BF16)
    C.dqT = S.sb("dqT", [128, 4, 2, 128], BF16)
    C.dk = S.sb("dk", [128, 4, 2])
    with S.phase():
        lgb = S.sb("lgb", [128, 8])
        cst = S.sb("rcst", [128, 6, 128])
        pc = S.sb("pcol", [128, 2])
        S.dma(lgb[:], D["lgb"][:])
        S.dma(cst[:], D["rcst"][:])
        S.dma(pc[:], D["pcol"][:])
        S.act(lgb[:], lgb[:], AF.Sigmoid)
        S.act(C.lg[:], lgb[:], AF.Ln)
        S.act(C.gC[:], C.lg[:], AF.Exp, scale=128.0)
        t1 = S.sb("t1", [128, 128])
        t2 = S.sb("t2", [128, 128])
        for h in range(4):
            S.act(t1[:], cst[:, 2, :], AF.Exp, scale=C.lg[:, h:h + 1])
            S.act(t2[:], cst[:, 3, :], AF.Exp, scale=C.lg[:, 4 + h:5 + h])
            S.tt("dve", t1[:], t1[:], cst[:, 4, :], ALU.mult)
            S.tt("dve", t2[:], t2[:], cst[:, 5, :], ALU.mult)
            S.tt("dve", C.Mc[:, h, :], t1[:], t2[:], ALU.add)
            S.act(C.dqT[:, h, 0, :], cst[:, 0, :], AF.Exp, scale=C.lg[:, h:h + 1])
            S.act(C.dqT[:, h, 1, :], cst[:, 1, :], AF.Exp, scale=C.lg[:, 4 + h:5 + h])
            S.act(C.dk[:, h, 0:1], pc[:, 0:1], AF.Exp, scale=C.lg[:, h:h + 1])
            S.act(C.dk[:, h, 1:2], pc[:, 1:2], AF.Exp, scale=C.lg[:, 4 + h:5 + h])


def retention_phase1(S, D, C):
    with S.phase():
        C.ulT = S.sb("ulT", [128, 8, UC], BF16)
        S.dma(C.ulT[:], D["ulT_sp"][:])
        _retention_phase1(S, D, C)


def _retention_phase1(S, D, C):
    ret_tables(S, D, C)
    win3 = D["w_in"].t.rearrange("(kc p) c -> p kc c", p=128)
    wrot3 = D["w_rot"].t.rearrange("(kc p) c -> p kc c", p=128)
    with S.phase():
        qT = S.sb("qT", [128, 4, 2048], BF16)
        kT = S.sb("kT", [128, 4, 2048], BF16)
        kcT = S.sb("kcT", [128, 4, 256], BF16)
        vtok = S.sb("vtok", [128, 18, 512], BF16)
        S_at = S.sb("S_at", [128, 16, 2, 4, 128], BF16)
        with S.phase():
            cos = S.sb("cos", [128, 2048])
            sin = S.sb("sin", [128, 2048])
            cosk = S.sb("cosk", [128, 2048])
            sink = S.sb("sink", [128, 2048])
            S.dma(cos[:], D["cosT"][:])
            S.dma(sin[:], D["sinT"][:])
            S.act(cosk[:], cos[:], AF.Copy, scale=K_SCALE)
            S.act(sink[:], sin[:], AF.Copy, scale=K_SCALE)
            stg = [S.sb("rstg%d" % i, [128, 8, 128]) for i in range(4)]
            wb = [S.sb("rwb%d" % i, [128, 8, 128], BF16) for i in range(4)]
            psA = [S.ps("psA%d" % i, [128, 512]) for i in range(2)]
            psB = [S.ps("psB%d" % i, [128, 512]) for i in range(2)]
            t1 = [S.sb("rt1_%d" % i, [128, 512]) for i in range(2)]
            t2 = [S.sb("rt2_%d" % i, [128, 512]) for i in range(2)]
            it = 0
            nl = 0
            for which, dst, ct, st in (("q", qT, cos, sin), ("k", kT, cosk, sink)):
                base = QC if which == "q" else KC
                rbase = 0 if which == "q" else 512
                for h in range(4):
                    k2 = nl % 2
                    nl += 1
                    load_w_chunk(S, D["w_in"], win3, base + h * 128, stg[2 * k2], wb[2 * k2])
                    load_w_chunk(S, D["w_rot"], wrot3, rbase + h * 128, stg[2 * k2 + 1], wb[2 * k2 + 1])
                    for g in range(4):
                        pa, pb = psA[it % 2], psB[it % 2]
                        c0 = LAT0 + g * 512
                        for kc in range(8):
                            S.mm(pa[:], wb[2 * k2][:, kc, :], C.ulT[:, kc, c0:c0 + 512], start=(kc == 0), stop=(kc == 7))
                        for kc in range(8):
                            S.mm(pb[:], wb[2 * k2 + 1][:, kc, :], C.ulT[:, kc, c0:c0 + 512], start=(kc == 0), stop=(kc == 7))
                        a, b_ = t1[it % 2], t2[it % 2]
                        S.tt("dve", a[:], pa[:], ct[:, g * 512:(g + 1) * 512], ALU.mult)
                        S.tt("dve", b_[:], pb[:], st[:, g * 512:(g + 1) * 512], ALU.mult)
                        S.tt("pool", dst[:, h, g * 512:(g + 1) * 512], a[:], b_[:], ALU.add)
                        it += 1
                    if which == "k":
                        pa = psA[it % 2]
                        for kc in range(8):
                            S.mm(pa[:, :256], wb[2 * k2][:, kc, :], C.ulT[:, kc, CTX0:CTX0 + 256], start=(kc == 0), stop=(kc == 7))
                        S.act(kcT[:, h, :], pa[:, :256], AF.Copy, scale=K_SCALE)
                        it += 1
            wv = S.sb("wv", [128, 8, 512], BF16)
            for c4 in range(4):
                load_w_chunk(S, D["w_in"], win3, VC + c4 * 128, stg[c4 % 4], wb[c4 % 4])
                S.copy("pool", wv[:, :, c4 * 128:(c4 + 1) * 128], wb[c4 % 4][:])
            for t in range(18):
                c0 = LAT0 + t * 128 if t < 16 else CTX0 + (t - 16) * 128
                pa = psA[t % 2]
                for kc in range(8):
                    S.mm(pa[:], C.ulT[:, kc, c0:c0 + 128], wv[:, kc, :], start=(kc == 0), stop=(kc == 7))
                S.act(vtok[:, t, :], pa[:], AF.Copy)
        S.dma(D["qT_sp"][:], qT[:])
        with S.phase():
            pk = [S.ps("pkt%d" % i, [128, 4, 128], BF16) for i in range(2)]
            pkv = [S.ps("pkv%d" % i, [128, 4, 128]) for i in range(2)]
            ktk = [S.sb("ktk%d" % i, [128, 4, 128], BF16) for i in range(2)]
            Sst = [S.sb("Sst%d" % i, [128, 4, 128]) for i in range(2)]
            Sctx = [S.sb("Sctx%d" % i, [128, 4, 128]) for i in range(2)]

            def kv(src, c, tile, dr, n):
                p = pk[n % 2]
                for h in range(4):
                    S.tr(p[:, h, :], src[:, h, c * 128:(c + 1) * 128], C.identb[:])
                kk_ = ktk[n % 2]
                for h in range(4):
                    if h % 2 == 0:
                        S.ts("dve", kk_[:, h, :], p[:, h, :], C.dk[:, h, dr:dr + 1])
                    else:
                        S.act(kk_[:, h, :], p[:, h, :], AF.Copy, scale=C.dk[:, h, dr:dr + 1])
                pv_ = pkv[n % 2]
                for h in range(4):
                    S.mm(pv_[:, h, :], kk_[:, h, :], vtok[:, tile, h * 128:(h + 1) * 128])
                return pv_

            n = 0
            for dr in range(2):
                for (src, nchunk, tile0, St, is_lat) in ((kcT, 2, 16, Sctx[dr], False), (kT, 16, 0, Sst[dr], True)):
                    S.memset("pool", St[:], 0.0)
                    order = range(nchunk) if dr == 0 else range(nchunk - 1, -1, -1)
                    for c in order:
                        if is_lat:
                            S.copy("pool", S_at[:, c, dr, :, :], St[:])
                        pv_ = kv(src, c, tile0 + c, dr, n)
                        n += 1
                        for h in range(4):
                            S.stt("dve", St[:, h, :], St[:, h, :], C.gC[:, dr * 4 + h:dr * 4 + h + 1], pv_[:, h, :], ALU.mult, ALU.add)
                S.dma(D["st_ret"][dr, 0], Sst[dr][:])
                S.dma(D["st_ret"][dr, 1], Sctx[dr][:])
        with S.phase():
            pS = [S.ps("pS%d" % i, [128, 4, 128]) for i in range(2)]
            pY = [S.ps("pY%d" % i, [128, 4, 128]) for i in range(2)]
            sm = [S.sb("sm%d" % i, [128, 4, 128], BF16) for i in range(2)]
            qp = [S.sb("qp%d" % i, [128, 4, 2, 128], BF16) for i in range(2)]
            yt = [S.sb("yt%d" % i, [128, 4, 128]) for i in range(2)]
            for c in range(16):
                cs = slice(c * 128, (c + 1) * 128)
                p = pS[c % 2]
                for h in range(4):
                    S.mm(p[:, h, :], kT[:, h, cs], qT[:, h, cs])
                S.tt("dve", sm[c % 2][:], p[:], C.Mc[:], ALU.mult)
                for dr in range(2):
                    S.tt("pool", qp[c % 2][:, :, dr, :], qT[:, :, cs], C.dqT[:, :, dr, :], ALU.mult)
                py = pY[c % 2]
                for h in range(4):
                    S.mm(py[:, h, :], sm[c % 2][:, h, :], vtok[:, c, h * 128:(h + 1) * 128], start=True, stop=False)
                    S.mm(py[:, h, :], qp[c % 2][:, h, 0, :], S_at[:, c, 0, h, :], start=False, stop=False)
                    S.mm(py[:, h, :], qp[c % 2][:, h, 1, :], S_at[:, c, 1, h, :], start=False, stop=True)
                S.act(yt[c % 2][:], py[:], AF.Copy)
                S.dma(D["yret_sp"][c * 128:(c + 1) * 128, :], yt[c % 2][:].re("p h e -> p (h e)"))


C0W = -float(np.exp(-0.5))
BLK = 128


def prep_rw_common(inp):
    d = {}
    g = lambda n: np.asarray(inp[n][0], np.float32)
    fm4 = lambda v: np.ascontiguousarray(v.reshape(4, 128).T)
    tab = np.zeros((128, 64), np.float32)
    mu = g("rwkv_mu_rkv")
    tab[:, 0:4], tab[:, 4:8], tab[:, 8:12] = fm4(mu[0]), fm4(mu[1]), fm4(mu[2])
    tab[:, 12:16], tab[:, 16:20] = fm4(g("rwkv_w0")[0]), fm4(g("rwkv_w0")[1])
    tab[:, 20:24], tab[:, 24:28] = fm4(g("rwkv_a0")[0]), fm4(g("rwkv_a0")[1])
    tab[:, 28:32] = fm4(g("rwkv_k_k"))
    tab[:, 32:36] = fm4(g("rwkv_k_a"))
    tab[:, 36:40] = fm4(g("rwkv_r_k").reshape(512))
    mx = g("rwkv_mu_x")
    for i in range(3):
        tab[:, 40 + 8 * i:48 + 8 * i] = fm(mx[i])
    d["rwtab"] = tab
    d["w1s"] = np.ascontiguousarray(np.concatenate([g("rwkv_w1")[0], g("rwkv_w1")[1]], 1))
    d["a1s"] = np.ascontiguousarray(np.concatenate([g("rwkv_a1")[0], g("rwkv_a1")[1]], 1))
    d["g1"] = g("rwkv_g1")
    d["w2s"] = np.ascontiguousarray(g("rwkv_w2").transpose(1, 0, 2))
    d["a2s"] = np.ascontiguousarray(g("rwkv_a2").transpose(1, 0, 2))
    d["g2"] = g("rwkv_g2")
    p = np.arange(64)[:, None]
    f = np.arange(64)[None, :]
    cm = np.zeros((64, 4, 8, 64), np.float32)
    cm[:, 0] = (p < f)[:, None, :]
    cm[:, 1] = (p <= f)[:, None, :]
    cm[:, 2] = (p > f)[:, None, :]
    cm[:, 3] = (p == f)[:, None, :]
    d["cmask"] = cm
    bo = np.zeros((128, 128), np.float32)
    bo[:64, :64] = 1
    bo[64:, 64:] = 1
    d["bones"] = bo
    return d


IN_SHAPES.update({"rwtab": [128, 64], "w1s": [1024, 64], "a1s": [1024, 64], "g1": [1024, 96], "w2s": [32, 2, 512],
                  "a2s": [32, 2, 512], "g2": [96, 512], "cmask": [64, 4, 8, 64], "bones": [128, 128]})


def rwkv_phase1(S, D, C):
    win3 = D["w_in"].t.rearrange("(kc p) c -> p kc c", p=128)
    with S.phase():
        tab = S.sb("rwtab", [128, 64])
        S.dma(tab[:], D["rwtab"][:])
        omka = S.sb("omka", [128, 4])
        S.ts("dve", omka[:], tab[:, 32:36], -1.0, 1.0, ALU.mult, ALU.add)
        cmask = S.sb("cmask", [64, 4, 8, 64])
        S.dma(cmask[:], D["cmask"][:])
        M_LT, M_LE, M_GT, M_EQ = (cmask[:, i] for i in range(4))
        bones = S.sb("bones", [128, 128])
        S.dma(bones[:], D["bones"][:])

        def load_bf(name, src, shape):
            b = S.sb(name, shape, BF16)
            with S.phase():
                st = S.sb(name + "_f", shape)
                S.dma(st[:], src)
                S.copy("pool", b[:], st[:])
            return b
        w1b = load_bf("w1b", View(D["w1s"], D["w1s"].t.rearrange("(kc p) r -> p kc r", p=128)), [128, 8, 64])
        a1b = load_bf("a1b", View(D["a1s"], D["a1s"].t.rearrange("(kc p) r -> p kc r", p=128)), [128, 8, 64])
        g1b = load_bf("g1b", View(D["g1"], D["g1"].t.rearrange("(kc p) r -> p kc r", p=128)), [128, 8, 96])
        w2b = load_bf("w2b", D["w2s"][:], [32, 2, 512])
        a2b = load_bf("a2b", D["a2s"][:], [32, 2, 512])
        g2b = load_bf("g2b", D["g2"][:], [96, 512])
        wr = S.sb("wr", [128, 8, 1536], BF16)
        with S.phase():
            wst = [S.sb("wst%d" % i, [128, 8, 128]) for i in range(2)]
            for c12 in range(12):
                S.dma(wst[c12 % 2][:], View(D["w_in"], win3[:, :, RC + c12 * 128:RC + (c12 + 1) * 128]))
                S.copy("pool", wr[:, :, c12 * 128:(c12 + 1) * 128], wst[c12 % 2][:])

        NB = BLK
        W = NB + 2
        class Slot:
            pass
        slots = []
        for dr in range(2):
            s = Slot()
            s.dr = dr
            s.sets = []
            for par in range(2):
                a = Slot()
                a.EI = S.sb("EI%d_%d" % (dr, par), [128, 4, NB])
                for nm in ("bt", "kt", "v"):
                    setattr(a, nm, S.sb("%s%d_%d" % (nm, dr, par), [128, 4, NB], BF16))
                for nm in ("atP", "rtP"):
                    setattr(a, nm, S.sb("%s%d_%d" % (nm, dr, par), [128, 2, 4, NB], BF16))
                    S.memset("pool", getattr(a, nm)[:], 0.0)
                s.sets.append(a)
            for nm in ("X", "XT", "P", "Aak", "Arb", "Ark", "X2", "XT2"):
                setattr(s, nm, S.sb("%s%d" % (nm, dr), [64, 8, 64], BF16))
            for nm in ("Vx", "BP", "KP", "Wsb", "Usb"):
                setattr(s, nm, S.sb("%s%d" % (nm, dr), [64, 8, 128], BF16))
            s.Tb = S.sb("Tb%d" % dr, [128, 4, 128], BF16)
            s.Tcb = S.sb("Tcb%d" % dr, [128, 4, 128], BF16)
            s.ysb = S.sb("ysb%d" % dr, [128, 8, 64])
            s.T = S.sb("T%d" % dr, [128, 4, 128])
            s.Tc = S.sb("Tc%d" % dr, [128, 4, 128])
            for nm in ("Vx", "BP", "KP"):
                S.memset("pool", getattr(s, nm)[:], 0.0)
            slots.append(s)
        ub = S.sb("ub", [128, 8, W], BF16)
        xb = [S.sb("xb%d" % i, [128, 8, NB], BF16) for i in range(3)]
        hw = S.sb("hw", [32, NB], BF16)
        ha = S.sb("ha", [32, NB], BF16)
        hg = S.sb("hg", [96, NB], BF16)
        FQ = [[S.sb("F%d_%d" % (i, q), [128, W]) for i in range(11)] for q in range(4)]
        pp = [S.ps("pp%d" % i, [128, 512]) for i in range(2)]
        pa = [S.ps("pa%d" % i, [64, 512]) for i in range(2)]
        pw = [S.ps("pw%d" % i, [64, 8, 128]) for i in range(1)]
        py = S.ps("py", [128, 8, 64])
        pt = S.ps("pt", [128, 4, 128])

        def run_jobs_nolimit(gens):
            gens = list(gens)
            while gens:
                nxt = []
                for g in gens:
                    try:
                        next(g)
                        nxt.append(g)
                    except StopIteration:
                        pass
                gens = nxt

        def prep(s, a, c0, n, is_lat):
            dr = s.dr
            S.dma(ub[:, :, :n + 2], D["ulT_sp"][:, :, c0 - 1:c0 + n + 1])
            t_, du = xb[0], xb[1]
            S.tt("dve", t_[:, :, :n], ub[:, :, 0:n], ub[:, :, 2:n + 2], ALU.add)
            S.stt("dve", du[:, :, :n], t_[:, :, :n], 0.5, ub[:, :, 1:n + 1], ALU.mult, ALU.subtract)
            xm = xb[2]

            def mix(i):
                S.tt("dve", t_[:, :, :n], du[:, :, :n], tab[:, 40 + 8 * i:48 + 8 * i].re("p (k o) -> p k o", o=1).bc([128, 8, n]), ALU.mult)
                S.tt("pool", xm[:, :, :n], t_[:, :, :n], ub[:, :, 1:n + 1], ALU.add)
            mix(0)
            for kc in range(8):
                S.mm(pp[0][:32, :n], w1b[:, kc, dr * 32:(dr + 1) * 32], xm[:, kc, :n], start=(kc == 0), stop=(kc == 7))
            S.act(hw[:, :n], pp[0][:32, :n], AF.Tanh)
            yield
            mix(1)
            for kc in range(8):
                S.mm(pp[1][:32, :n], a1b[:, kc, dr * 32:(dr + 1) * 32], xm[:, kc, :n], start=(kc == 0), stop=(kc == 7))
            S.act(ha[:, :n], pp[1][:32, :n], AF.Copy)
            yield
            do_g = is_lat and dr == 0
            if do_g:
                mix(2)
                for kc in range(8):
                    S.mm(pp[0][:96, :n], g1b[:, kc, :], xm[:, kc, :n], start=(kc == 0), stop=(kc == 7))
                S.act(hg[:, :n], pp[0][:96, :n], AF.Sigmoid)
            blk = (c0 - LAT0) // NB if is_lat else None
            ipc = [0]

            def q4job(q4):
                F = FQ[q4]
                pb = pp[q4 % 2][:]
                pr, pk_, pv_ = F[0], F[1], F[2]
                for i3, dst in enumerate((pr, pk_, pv_)):
                    p_ = pb
                    for kc in range(8):
                        S.mm(p_[:, :n + 2], wr[:, kc, i3 * 512 + q4 * 128:i3 * 512 + (q4 + 1) * 128], ub[:, kc, :n + 2],
                             start=(kc == 0), stop=(kc == 7))
                    S.act(dst[:, :n + 2], p_[:, :n + 2], AF.Copy)
                    yield
                outs = (F[5], F[6], a.v[:, q4, :n])
                for i3, (src, dst) in enumerate(zip((pr, pk_, pv_), outs)):
                    dv = dst if isinstance(dst, View) else dst[:, :n]
                    S.tt("pool", F[3][:, :n], src[:, 0:n], src[:, 2:n + 2], ALU.add)
                    yield
                    S.stt("dve", F[4][:, :n], F[3][:, :n], 0.5, src[:, 1:n + 1], ALU.mult, ALU.subtract)
                    yield
                    S.stt("dve", dv, F[4][:, :n], tab[:, 4 * i3 + q4:4 * i3 + q4 + 1], src[:, 1:n + 1], ALU.mult, ALU.add)
                    yield
                r_, k_ = F[5], F[6]
                S.ts("dve", F[3][:, :n], k_[:, :n], tab[:, 28 + q4:29 + q4])
                yield
                S.tt("pool", F[4][:, :n], F[3][:, :n], F[3][:, :n], ALU.mult)
                yield
                p_ = pb
                S.mm(p_[:, :n], bones[:], F[4][:, :n])
                yield
                S.act(F[4][:, :n], p_[:, :n], AF.Sqrt)
                yield
                S.ts("dve", F[4][:, :n], F[4][:, :n], 1e-12, op0=ALU.max)
                yield
                S.op("dve", lambda e, a=F[4].t[:, :n]: e.reciprocal(out=a, in_=a), reads=[F[4]], writes=[F[4]])
                yield
                kk = F[7]
                S.tt("dve", kk[:, :n], F[3][:, :n], F[4][:, :n], ALU.mult)
                yield
                p_ = pb
                S.mm(p_[:, :n], w2b[:, dr, q4 * 128:(q4 + 1) * 128], hw[:, :n])
                yield
                lw = F[1]
                S.act(lw[:, :n], p_[:, :n], AF.Sigmoid, bias=tab[:, 12 + 4 * dr + q4:13 + 4 * dr + q4])
                yield
                S.ts("dve", lw[:, :n], lw[:, :n], C0W)
                yield
                p_ = pb
                S.mm(p_[:, :n], a2b[:, dr, q4 * 128:(q4 + 1) * 128], ha[:, :n])
                yield
                asg = F[0]
                S.act(asg[:, :n], p_[:, :n], AF.Sigmoid, bias=tab[:, 20 + 4 * dr + q4:21 + 4 * dr + q4])
                yield
                keff = F[2]
                S.ts("dve", keff[:, :n], asg[:, :n], tab[:, 32 + q4:33 + q4], omka[:, q4:q4 + 1], ALU.mult, ALU.add)
                yield
                S.tt("dve", keff[:, :n], keff[:, :n], k_[:, :n], ALU.mult)
                yield
                bb = F[3]
                S.tt("pool", bb[:, :n], kk[:, :n], asg[:, :n], ALU.mult)
                yield
                if do_g:
                    S.stt("dve", F[4][:, :n], r_[:, :n], tab[:, 36 + q4:37 + q4], keff[:, :n], ALU.mult, ALU.mult)
                    p_ = pb
                    S.mm(p_[:, :n], bones[:], F[4][:, :n])
                    S.tt("dve", F[4][:, :n], p_[:, :n], a.v[:, q4, :n], ALU.mult)
                    S.dma(D["bonusT_sp"][:, q4, blk * NB:blk * NB + n], F[4][:, :n])
                    p_ = pb
                    S.mm(p_[:, :n], g2b[:, q4 * 128:(q4 + 1) * 128], hg[:, :n])
                    S.act(F[10][:, :n], p_[:, :n], AF.Copy)
                    S.dma(D["gT_sp"][:, q4, blk * NB:blk * NB + n], F[10][:, :n])
                A_, B_ = lw, F[8]
                nch = n // 64
                va = lambda T_: T_[:, :n].re("p (c t) -> p c t", t=64)
                src_, dst_ = A_, B_
                for dsh in (1, 2, 4, 8, 16, 32):
                    a3, b3 = va(src_), va(dst_)
                    if dr == 0:
                        S.tt("pool", b3[:, :, dsh:], a3[:, :, dsh:], a3[:, :, :64 - dsh], ALU.add)
                        S.copy("pool", b3[:, :, :dsh], a3[:, :, :dsh])
                    else:
                        S.tt("pool", b3[:, :, :64 - dsh], a3[:, :, :64 - dsh], a3[:, :, dsh:], ALU.add)
                        S.copy("pool", b3[:, :, 64 - dsh:], a3[:, :, 64 - dsh:])
                    yield
                    if dsh == 1:
                        src_, dst_ = B_, F[9]
                    else:
                        src_, dst_ = dst_, src_
                cumI = src_
                cumX = dst_
                S.tt("dve", cumX[:, :n], cumI[:, :n], lw[:, :n], ALU.subtract)
                yield
                S.act(a.EI[:, q4, :n], cumI[:, :n], AF.Exp)
                yield
                EX, EN = F[0], F[6]
                S.act(EX[:, :n], cumX[:, :n], AF.Exp)
                yield
                S.act(EN[:, :n], cumI[:, :n], AF.Exp, scale=-1.0)
                yield
                for e in range(2):
                    rw_ = slice(e * 64, (e + 1) * 64)
                    S.stt("dve", a.atP[rw_, e, q4, :n], kk[rw_, :n], -1.0, EX[rw_, :n], ALU.mult, ALU.mult)
                S.tt("dve", a.bt[:, q4, :n], bb[:, :n], EN[:, :n], ALU.mult)
                yield
                S.tt("pool", a.kt[:, q4, :n], keff[:, :n], EN[:, :n], ALU.mult)
                yield
                if is_lat:
                    for e in range(2):
                        rw_ = slice(e * 64, (e + 1) * 64)
                        S.tt("dve", a.rtP[rw_, e, q4, :n], r_[rw_, :n], (a.EI[rw_, q4, :n] if dr == 0 else EX[rw_, :n]), ALU.mult)

            for pair in ((0, 1), (2, 3)):
                gens = [q4job(q) for q in pair]
                while gens:
                    nxt = []
                    for g in gens:
                        try:
                            next(g)
                            nxt.append(g)
                        except StopIteration:
                            pass
                    gens = nxt
                    yield

        def chain(s, a, m, T, nn, with_y, ychunk):
            dr = s.dr
            Tb = s.Tb if T is s.T else s.Tcb
            cs = slice(m * 64, (m + 1) * 64)
            hv = lambda arr, h: (arr[:, h % 2, h // 2, cs] if arr is a.atP or arr is a.rtP else arr[:, h // 2, cs])
            ms = M_LT if dr == 0 else M_GT
            msT = M_GT if dr == 0 else M_LT
            mr = M_LE if dr == 0 else M_GT
            f2 = lambda t_: t_[:].re("p h t -> p (h t)")
            p_ = View(pa[0], pa[0].t[:].bitcast(BF16))[:, 0:512]
            for q4 in range(4):
                S.tr(p_[:, q4 * 128:(q4 + 1) * 128], a.v[:, q4, cs], C.identb[:])
            S.copy("dve", s.Vx[:, :, 0:64], p_.re("p (h k) -> p h k", k=64))
            for src, dstp in ((a.bt, s.BP), (a.kt, s.KP)):
                p_ = View(pa[1], pa[1].t[:].bitcast(BF16))[:, 0:512]
                for q4 in range(4):
                    S.tr(p_[:, q4 * 128:(q4 + 1) * 128], src[:, q4, cs], C.identb[:])
                p4 = p_.re("p (q e k) -> p q e k", e=2, k=64)
                d4 = dstp[:].re("p (q e) k -> p q e k", e=2)
                S.copy("dve", d4[:, :, 0, 0:64], p4[:, :, 0, :])
                S.act(d4[:, :, 1, 64:128], p4[:, :, 1, :], AF.Copy)
            yield
            def amat(lhs, rhs, mask, dst, pi, eng):
                p_ = pa[pi]
                for h in range(8):
                    S.mm(p_[:, h * 64:(h + 1) * 64], hv(lhs, h), hv(rhs, h))
                S.tt(eng, f2(dst), p_[:], mask.re("p h t -> p (h t)"), ALU.mult)
            amat(a.bt, a.atP, ms, s.X, 0, "dve")
            amat(a.atP, a.bt, msT, s.XT, 1, "dve")
            amat(a.kt, a.atP, ms, s.Aak, 0, "dve")
            if with_y:
                amat(a.bt, a.rtP, mr, s.Arb, 1, "dve")
                amat(a.kt, a.rtP, mr, s.Ark, 0, "dve")
            S.tt("pool", f2(s.P), f2(s.X), M_EQ.re("p h t -> p (h t)"), ALU.add)
            yield
            X, XT, X2, XT2 = s.X, s.XT, s.X2, s.XT2
            for k in range(5):
                if k < 4:
                    for h in range(8):
                        S.mm(pa[0][:, h * 64:(h + 1) * 64], XT[:, h, :], X[:, h, :])
                for h in range(8):
                    S.mm(pa[1][:, h * 64:(h + 1) * 64], X[:, h, :], XT[:, h, :])
                if k < 4:
                    S.act(f2(X2), pa[0][:], AF.Copy)
                S.copy("dve", f2(XT2), pa[1][:])
                X, X2 = X2, X
                XT, XT2 = XT2, XT
                for h in range(8):
                    S.mm(pa[0][:, h * 64:(h + 1) * 64], XT[:, h, :], s.P[:, h, :])
                S.tt("dve", f2(s.P), f2(s.P), pa[0][:], ALU.add)
                yield
            pw_ = pw[0]
            for h in range(8):
                S.mm(pw_[:, h, :nn], hv(a.atP, h), Tb[:, h // 2, :nn], start=True, stop=False)
                S.mm(pw_[:, h, :nn], s.Aak[:, h, :], s.Vx[:, h, :nn], start=False, stop=True)
            S.act(s.Wsb[:, 0:4, :nn], pw_[:, 0:4, :nn], AF.Copy)
            S.copy("dve", s.Wsb[:, 4:8, :nn], pw_[:, 4:8, :nn])
            yield
            for h in range(8):
                S.mm(pw_[:, h, :nn], s.P[:, h, :], s.Wsb[:, h, :nn])
            S.act(s.Usb[:, 0:4, :nn], pw_[:, 0:4, :nn], AF.Copy)
            S.copy("dve", s.Usb[:, 4:8, :nn], pw_[:, 4:8, :nn])
            yield
            if with_y:
                for h in range(8):
                    S.mm(py[:, h, :], Tb[:, h // 2, :], hv(a.rtP, h), start=True, stop=False)
                    S.mm(py[:, h, :], s.Usb[:, h, :], s.Arb[:, h, :], start=False, stop=False)
                    S.mm(py[:, h, :], s.Vx[:, h, :], s.Ark[:, h, :], start=False, stop=True)
                S.act(s.ysb[:], py[:], AF.Copy)
                S.dma(D["yext_sp"][dr, ychunk], s.ysb[:].re("p h t -> p (h t)"))
            for q4 in range(4):
                for e in range(2):
                    h = 2 * q4 + e
                    S.mm(pt[:, q4, :nn], s.BP[:, h, :], s.Usb[:, h, :nn], start=(e == 0), stop=False)
                    S.mm(pt[:, q4, :nn], s.KP[:, h, :], s.Vx[:, h, :nn], start=False, stop=(e == 1))
            gi = m * 64 + (63 if dr == 0 else 0)
            S.tt("dve", T[:, :, :nn], T[:, :, :nn], pt[:, :, :nn], ALU.add)
            for q4 in range(4):
                S.ts("dve", T[:, q4, :nn], T[:, q4, :nn], a.EI[:, q4, gi:gi + 1])
                S.act(Tb[:, q4, :nn], T[:, q4, :nn], AF.Copy)
            yield

        CH = [int(os.environ.get("CH_STOP", "1000000"))]

        def run_jobs(gens):
            gens = list(gens)
            while gens:
                pass
                nxt = []
                for g in gens:
                    try:
                        next(g)
                        nxt.append(g)
                    except StopIteration:
                        pass
                gens = nxt

        def block_job(s, a, c0, n, is_lat, T, nn, blk):
            nch = n // 64
            order = range(nch) if s.dr == 0 else range(nch - 1, -1, -1)
            for m in order:
                yc = (blk * (BLK // 64) + m) if is_lat else None
                yield from chain(s, a, m, T, nn, is_lat, yc)

        for s in slots:
            S.memset("pool", s.Tc[:], 0.0)
            S.memset("pool", s.T[:], 0.0)
            S.copy("pool", s.T[0:64, :, 64:128], C.ident[0:64, 0:64].re("p (o k) -> p o k", o=1).bc([64, 4, 64]))
            S.copy("pool", s.T[64:128, :, 64:128], C.ident[64:128, 64:128].re("p (o k) -> p o k", o=1).bc([64, 4, 64]))
            S.copy("pool", s.Tb[:], s.T[:])
            S.copy("pool", s.Tcb[:], s.Tc[:])
        ncb = 256 // BLK
        for i in range(ncb):
            run_jobs([prep(slots[0], slots[0].sets[0], CTX0 + i * BLK, BLK, False)])
            run_jobs([prep(slots[1], slots[1].sets[0], CTX0 + (ncb - 1 - i) * BLK, BLK, False)])
            run_jobs([block_job(slots[0], slots[0].sets[0], 0, BLK, False, slots[0].Tc, 64, None),
                      block_job(slots[1], slots[1].sets[0], 0, BLK, False, slots[1].Tc, 64, None)])
        for s in slots:
            S.dma(D["st_rwc"][s.dr], s.Tc[:, :, 0:64])
        nblk = 2048 // BLK

        def prep_pair(i):
            yield from prep(slots[0], slots[0].sets[i % 2], LAT0 + i * BLK, BLK, True)
            yield from prep(slots[1], slots[1].sets[i % 2], LAT0 + (nblk - 1 - i) * BLK, BLK, True)

        run_jobs([prep_pair(0)])
        for i in range(nblk):
            bf, bb_ = i, nblk - 1 - i
            jobs = [block_job(slots[0], slots[0].sets[i % 2], 0, BLK, True, slots[0].T, 128, bf),
                    block_job(slots[1], slots[1].sets[i % 2], 0, BLK, True, slots[1].T, 128, bb_)]
            if i + 1 < nblk:
                jobs.append(prep_pair(i + 1))
            run_jobs(jobs)
        for s in slots:
            S.dma(D["st_rw"][s.dr], s.T[:])


RET_EPS = 1e-5
RW_EPS = 64e-5


def prep_p2_common(inp):
    d = {}
    g = lambda n: np.asarray(inp[n][0], np.float32)
    d["w_ret_o"] = g("w_ret_o")
    d["w_rwkv_o"] = g("w_rwkv_o")
    d["w_out"] = g("w_out")
    d["lnwb"] = np.ascontiguousarray(np.broadcast_to(np.stack([g("rwkv_ln_w"), g("rwkv_ln_b")])[None], (128, 2, 512))).astype(np.float32)
    c = np.arange(16, dtype=np.float32)
    d["cpos"] = np.ascontiguousarray(np.broadcast_to(np.stack([128 * c, 128 * (15 - c)])[None], (128, 2, 16))).astype(np.float32)
    d["f2_wg"] = inp["ffn2_w_gate"][0]
    d["f2_wu"] = inp["ffn2_w_up"][0]
    d["f2_wd"] = inp["ffn2_w_down"][0]
    d["gfin"] = np.ascontiguousarray(np.broadcast_to(np.asarray(inp["g_final"], np.float32)[None], (128, 1024)))
    return d


def prep_p2_core(r):
    b, j = r // 4, r % 4
    m = np.zeros((128, 2, 8), np.float32)
    for rr in range(8):
        if rr // 4 == b and rr < r:
            m[:, 0, rr] = 1
        if rr // 4 == b and rr > r:
            m[:, 1, rr] = 1
    return {"cmaskr": m}


P2_SHAPES = {"w_ret_o": [512, 1024], "w_rwkv_o": [512, 1024], "w_out": [1024, 1024], "lnwb": [128, 2, 512], "cpos": [128, 2, 16],
             "f2_wg": [1024, D_FF], "f2_wu": [1024, D_FF], "f2_wd": [D_FF, 1024], "gfin": [128, 1024], "cmaskr": [128, 2, 8]}


def load_w_bf(S, Dres, src3, ncols, dst, stg):
    for i, c0 in enumerate(range(0, ncols, 128)):
        st = stg[i % len(stg)]
        kcs = dst.t.shape[1]
        S.dma(st[:, :kcs, :], View(Dres, src3[:, :, c0:c0 + 128]))
        S.copy("pool", dst[:, :, c0:c0 + 128], st[:, :kcs, :])


def compose_states(S, D, C, own_rank_dram):
    C.Sin_b = [S.sb("Sinb%d" % i, [128, 4, 128], BF16) for i in range(2)]
    C.LH = [S.sb("LH%d" % i, [128, 8, 64]) for i in range(2)]
    with S.phase():
        mk = S.sb("mk", [128, 2, 8])
        S.dma(mk[:], D["cmaskr"][:])
        G = S.sb("G2048", [128, 8])
        S.act(G[:], C.lg[:], AF.Exp, scale=2048.0)
        Gm1 = S.sb("Gm1", [128, 8])
        S.ts("dve", Gm1[:], G[:], -1.0, op0=ALU.add)
        cf = S.sb("cf", [128, 2, 8, 4])
        for dr in range(2):
            for h in range(4):
                S.ts("dve", cf[:, dr, :, h], mk[:, dr, :], Gm1[:, dr * 4 + h:dr * 4 + h + 1], 1.0, ALU.mult, ALU.add)
        Sin = [S.sb("Sin%d" % i, [128, 4, 128]) for i in range(2)]
        ld = [S.sb("sld%d" % i, [128, 4, 128]) for i in range(2)]
        tmp = S.sb("stmp", [128, 4, 128])
        n = 0
        for dr in range(2):
            S.dma(Sin[dr][:], D["own_st_ret"][dr, 1])
            order = range(8) if dr == 0 else range(7, -1, -1)
            for rr in order:
                l = ld[n % 2]
                n += 1
                S.dma(l[:], D["all_st_ret"][rr, dr, 0])
                S.ts("dve", tmp[:], l[:], mk[:, dr, rr:rr + 1])
                for h in range(4):
                    S.stt("dve", Sin[dr][:, h, :], Sin[dr][:, h, :], cf[:, dr, rr, h:h + 1], tmp[:, h, :], ALU.mult, ALU.add)
            S.copy("dve", C.Sin_b[dr][:], Sin[dr][:])
        Tin = [S.sb("Tin%d" % i, [128, 4, 64]) for i in range(2)]
        tl = [S.sb("tld%d" % i, [128, 4, 128]) for i in range(2)]
        BD = S.sb("BD", [128, 128])
        BDT = S.sb("BDT", [128, 128])
        dd = S.sb("dd", [128, 64])
        S.memset("pool", BD[:], 0.0)
        pb = S.ps("pbd", [128, 128])
        pq = S.ps("pq", [128, 64])
        for dr in range(2):
            S.dma(Tin[dr][:], D["own_st_rwc"][dr])
            order = range(8) if dr == 0 else range(7, -1, -1)
            for rr in order:
                l = tl[n % 2]
                n += 1
                S.dma(l[:], D["all_st_rw"][rr, dr])
                for q4 in range(4):
                    S.copy("dve", BD[0:64, 0:64], l[0:64, q4, 64:128])
                    S.copy("dve", BD[64:128, 64:128], l[64:128, q4, 64:128])
                    S.tr(pb[:], BD[:], C.ident[:])
                    S.act(BDT[:], pb[:], AF.Copy)
                    S.mm(pq[:], BDT[:], Tin[dr][:, q4, :])
                    S.tt("dve", dd[:], pq[:], l[:, q4, 0:64], ALU.add)
                    S.tt("dve", dd[:], dd[:], Tin[dr][:, q4, :], ALU.subtract)
                    S.stt("dve", Tin[dr][:, q4, :], dd[:], mk[:, dr, rr:rr + 1], Tin[dr][:, q4, :], ALU.mult, ALU.add)
            S.copy("dve", C.LH[dr][0:64, :, :], C.ident[0:64, 0:64].re("p (o k) -> p o k", o=1).bc([64, 8, 64]))
            for q4 in range(4):
                S.copy("dve", C.LH[dr][64:128, 2 * q4 + 1, :], Tin[dr][64:128, q4, :])
                S.dma(C.LH[dr][64:128, 2 * q4, :], Tin[dr][0:64, q4, :])


def head_norm_tok(S, src, nh, hd, eps, scr, out_fn):
    s1, s2, sq, mean, var = scr
    S.op("dve", lambda e: e.tensor_reduce(out=s1.t[:, :nh], in_=src.ap, axis=AX.X, op=ALU.add), reads=[src], writes=[s1])
    S.tt("pool", sq[:, :nh, :hd], src, src, ALU.mult)
    S.op("dve", lambda e: e.tensor_reduce(out=s2.t[:, :nh], in_=sq.t[:, :nh, :hd], axis=AX.X, op=ALU.add), reads=[sq], writes=[s2])
    S.ts("dve", mean[:, :nh], s1[:, :nh], 1.0 / hd)
    S.tt("dve", var[:, :nh], mean[:, :nh], mean[:, :nh], ALU.mult)
    S.stt("dve", var[:, :nh], s2[:, :nh], 1.0 / hd, var[:, :nh], ALU.mult, ALU.subtract)
    S.ts("dve", var[:, :nh], var[:, :nh], eps, op0=ALU.add)
    S.act(var[:, :nh], var[:, :nh], AF.Sqrt)
    S.op("dve", lambda e: e.reciprocal(out=var.t[:, :nh], in_=var.t[:, :nh]), reads=[var], writes=[var])
    for h in range(nh):
        S.ts("dve", src[:, h, :], src[:, h, :], mean[:, h:h + 1], var[:, h:h + 1], ALU.subtract, ALU.mult)


def merge_phase2(S, D, C):
    C.lg = S.sb("lg", [128, 8])
    C.dqT = S.sb("dqT", [128, 4, 2, 128], BF16)
    gch = S.sb("gch", [128, 2, 4, 16])
    with S.phase():
        lgb = S.sb("lgb", [128, 8])
        cst = S.sb("rcst", [128, 6, 128])
        cpos = S.sb("cpos", [128, 2, 16])
        S.dma(lgb[:], D["lgb"][:])
        S.dma(cst[:], D["rcst"][:])
        S.dma(cpos[:], D["cpos"][:])
        S.act(lgb[:], lgb[:], AF.Sigmoid)
        S.act(C.lg[:], lgb[:], AF.Ln)
        for h in range(4):
            S.act(C.dqT[:, h, 0, :], cst[:, 0, :], AF.Exp, scale=C.lg[:, h:h + 1])
            S.act(C.dqT[:, h, 1, :], cst[:, 1, :], AF.Exp, scale=C.lg[:, 4 + h:5 + h])
            for dr in range(2):
                S.act(gch[:, dr, h, :], cpos[:, dr, :], AF.Exp, scale=C.lg[:, dr * 4 + h:dr * 4 + h + 1])
    compose_states(S, D, C, None)
    G5 = bcast_rows(S, C, "G5", 5, 0, 1.0)
    win3 = D["w_in"].t.rearrange("(kc p) c -> p kc c", p=128)
    with S.phase():
        wgr = S.sb("wgr", [128, 8, 512], BF16)
        wgt = S.sb("wgt", [128, 8, 2048], BF16)
        wro = S.sb("wro", [128, 4, 1024], BF16)
        wwo = S.sb("wwo", [128, 4, 1024], BF16)
        wout = S.sb("wout", [128, 8, 1024], BF16)
        lnwb = S.sb("lnwb", [128, 2, 512])
        S.dma(lnwb[:], D["lnwb"][:])
        with S.phase():
            stg = [S.sb("mstg%d" % i, [128, 8, 128]) for i in range(2)]
            load_w_bf(S, D["w_in"], win3[:, :, GC:GC + 512], 512, wgr, stg)
            load_w_bf(S, D["w_in"], win3[:, :, GRET:GRET + 2048], 2048, wgt, stg)
            load_w_bf(S, D["w_ret_o"], D["w_ret_o"].t.rearrange("(kc p) c -> p kc c", p=128), 1024, wro, stg)
            load_w_bf(S, D["w_rwkv_o"], D["w_rwkv_o"].t.rearrange("(kc p) c -> p kc c", p=128), 1024, wwo, stg)
            load_w_bf(S, D["w_out"], D["w_out"].t.rearrange("(kc p) c -> p kc c", p=128), 1024, wout, stg)
        ut = [S.sb("ut%d" % i, [128, 8, 128], BF16) for i in range(2)]
        qt = [S.sb("qt%d" % i, [128, 4, 128], BF16) for i in range(2)]
        qp = S.sb("qp", [128, 4, 2, 128], BF16)
        yl = [S.sb("yl%d" % i, [128, 4, 128]) for i in range(2)]
        ye = [S.sb("ye%d" % i, [128, 8, 64]) for i in range(4)]
        bn = [S.sb("bn%d" % i, [128, 4, 128]) for i in range(2)]
        gg = [S.sb("gg%d" % i, [128, 4, 128]) for i in range(2)]
        hh = [S.sb("hh%d" % i, [128, 1024]) for i in range(2)]
        scr = (S.sb("s1", [128, 8]), S.sb("s2", [128, 8]), S.sb("sq", [128, 8, 128]), S.sb("mean", [128, 8]), S.sb("var", [128, 8]))
        yrw = S.sb("yrw", [128, 8, 64])
        yT = S.sb("yT", [64, 8, 128])
        sgr = S.sb("sgr", [128, 512])
        sgt = S.sb("sgt", [128, 2048], BF16)
        ro = S.sb("ro", [128, 512], BF16)
        roT = S.sb("roT", [128, 4, 128], BF16)
        bnt = S.sb("bnt", [128, 512])
        ggt = S.sb("ggt", [128, 512])
        msum = S.sb("msum", [128, 1024])
        msb = S.sb("msb", [128, 1024], BF16)
        msT = S.sb("msT", [128, 8, 128], BF16)
        tmpo = S.sb("tmpo", [128, 512])
        h2 = [S.sb("h2_%d" % i, [128, 1024]) for i in range(2)]
        banks = [S.ps("bk%d" % i, [128, 512]) for i in range(5)]
        ptb = S.ps("ptb", [128, 8, 128], BF16)
        prw = S.ps("prw", [64, 8, 128])
        nbk = [0]

        def bank():
            nbk[0] += 1
            return banks[nbk[0] % 5]

        for t in range(16):
            c0 = LAT0 + t * 128
            u = ut[t % 2]
            S.dma(u[:], D["ulT_sp"][:, :, c0:c0 + 128])
            q = qt[t % 2]
            S.dma(q[:], D["qT_sp"][:, :, t * 128:(t + 1) * 128])
            y = yl[t % 2]
            S.dma(y[:].re("p h e -> p (h e)"), D["yret_sp"][t * 128:(t + 1) * 128, :])
            hcur = hh[t % 2]
            S.dma(hcur[:], D["h_sp"][t * 128:(t + 1) * 128, :])
            for dr in range(2):
                S.tt("pool", qp[:, :, dr, :], q[:], C.dqT[:, :, dr, :], ALU.mult)
            for dr in range(2):
                b = bank()
                bv = b[:].re("p (h e) -> p h e", h=4)
                for h in range(4):
                    S.mm(bv[:, h, :], qp[:, h, dr, :], C.Sin_b[dr][:, h, :])
                for h in range(4):
                    S.stt("dve", y[:, h, :], bv[:, h, :], gch[:, dr, h, t:t + 1], y[:, h, :], ALU.mult, ALU.add)
            def dbg(i, view, n):
                if t == 0 and "dbg" in D:
                    dt_ = S.sb("dbgt%d" % i, [128, 1024])
                    S.copy("dve", dt_[:, :n], view)
                    S.dma(D["dbg"][i, :, :n], dt_[:, :n])
            dbg(0, y[:].re("p h e -> p (h e)"), 512)
            head_norm_tok(S, y[:], 4, 128, RET_EPS, scr, None)
            dbg(5, y[:].re("p h e -> p (h e)"), 512)
            b = bank()
            for kc in range(8):
                S.mm(b[:], u[:, kc, :], wgr[:, kc, :], start=(kc == 0), stop=(kc == 7))
            S.act(sgr[:], b[:], AF.Silu)
            S.tt("dve", ro[:], y[:].re("p h e -> p (h e)"), sgr[:], ALU.mult)
            dbg(2, ro[:], 512)
            for f in range(4):
                S.tr(ptb[:, f, :], ro[:, f * 128:(f + 1) * 128], C.identb[:])
            S.copy("dve", roT[:], ptb[:, 0:4, :])
            for g4 in range(4):
                b = bank()
                for kc in range(8):
                    S.mm(b[:], u[:, kc, :], wgt[:, kc, g4 * 512:(g4 + 1) * 512], start=(kc == 0), stop=(kc == 7))
                S.act(sgt[:, g4 * 512:(g4 + 1) * 512], b[:], AF.Sigmoid)
            for hf in range(2):
                b = bank()
                for f in range(4):
                    S.mm(b[:], roT[:, f, :], wro[:, f, hf * 512:(hf + 1) * 512], start=(f == 0), stop=(f == 3))
                S.tt("dve", msum[:, hf * 512:(hf + 1) * 512], b[:], sgt[:, hf * 512:(hf + 1) * 512], ALU.mult)
            for ch in range(2):
                for dr in range(2):
                    S.dma(ye[2 * ch + dr][:].re("p h t -> p (h t)"), D["yext_sp"][dr, 2 * t + ch])
                for h in range(8):
                    for dr in range(2):
                        S.mm(prw[:, h, ch * 64:(ch + 1) * 64], C.LH[dr][:, h, :], ye[2 * ch + dr][:, h, :], start=(dr == 0), stop=(dr == 1))
            S.act(yT[:, 0:4, :], prw[:, 0:4, :], AF.Copy)
            S.copy("dve", yT[:, 4:8, :], prw[:, 4:8, :])
            b = bank()
            bv = b[:].re("p (h v) -> p h v", h=8)
            for h in range(8):
                S.tr(bv[:, h, :], yT[:, h, :], C.ident[0:64, 0:64])
            S.act(yrw[:], bv, AF.Copy)
            dbg(1, yrw[:].re("p h v -> p (h v)"), 512)
            head_norm_tok(S, yrw[:], 8, 64, RW_EPS, scr, None)
            bt_, gt_ = bn[t % 2], gg[t % 2]
            S.dma(bt_[:], D["bonusT_sp"][:, :, t * 128:(t + 1) * 128])
            S.dma(gt_[:], D["gT_sp"][:, :, t * 128:(t + 1) * 128])
            for src, dst in ((bt_, bnt), (gt_, ggt)):
                b = bank()
                for q4 in range(4):
                    S.tr(b[:, q4 * 128:(q4 + 1) * 128], src[:, q4, :], C.ident[:])
                S.act(dst[:], b[:], AF.Copy)
            yf = yrw[:].re("p h v -> p (h v)")
            S.tt("dve", yf, yf, lnwb[:, 0, :], ALU.mult)
            S.tt("pool", yf, yf, lnwb[:, 1, :], ALU.add)
            S.tt("dve", yf, yf, bnt[:], ALU.add)
            S.tt("dve", ro[:], yf, ggt[:], ALU.mult)
            dbg(3, ro[:], 512)
            for f in range(4):
                S.tr(ptb[:, f, :], ro[:, f * 128:(f + 1) * 128], C.identb[:])
            S.copy("dve", roT[:], ptb[:, 0:4, :])
            for hf in range(2):
                b = bank()
                for f in range(4):
                    S.mm(b[:], roT[:, f, :], wwo[:, f, hf * 512:(hf + 1) * 512], start=(f == 0), stop=(f == 3))
                S.tt("dve", tmpo[:], b[:], sgt[:, 1024 + hf * 512:1024 + (hf + 1) * 512], ALU.mult)
                S.tt("pool", msum[:, hf * 512:(hf + 1) * 512], msum[:, hf * 512:(hf + 1) * 512], tmpo[:], ALU.add)
            dbg(4, msum[:], 1024)
            S.act(msb[:], msum[:], AF.Copy)
            for f in range(8):
                S.tr(ptb[:, f, :], msb[:, f * 128:(f + 1) * 128], C.identb[:])
            S.copy("dve", msT[:], ptb[:])
            hn = h2[t % 2]
            for hf in range(2):
                b = bank()
                for f in range(8):
                    S.mm(b[:], msT[:, f, :], wout[:, f, hf * 512:(hf + 1) * 512], start=(f == 0), stop=(f == 7))
                S.tt("dve", tmpo[:], b[:], G5[:, hf * 512:(hf + 1) * 512], ALU.mult)
                S.tt("pool", hn[:, hf * 512:(hf + 1) * 512], tmpo[:], hcur[:, hf * 512:(hf + 1) * 512], ALU.add)
            S.dma(D["h2_sp"][t * 128:(t + 1) * 128, :], hn[:])


def mod_reload(S, D, C):
    C.ident = S.sb("ident", [128, 128])
    C.identb = S.sb("identb", [128, 128], BF16)
    S.dma(C.ident[:], D["ident"][:])
    S.copy("dve", C.identb[:], C.ident[:])
    C.modT = S.sb("modT", [128, 72, 2])
    S.dma(C.modT[:], D["modT_sp"][:])
    C.gv = S.sb("gv", [128, 3, 8])
    S.dma(C.gv[:], D["gvec"][:])
    t = S.sb("gs2", [128, 8, 2])
    for col in range(2):
        S.stt("dve", t[:, :, col], C.modT[:, 56:64, col], 1.0, C.gv[:, 2, :], ALU.add, ALU.mult)
    C.gs2 = t


def ffn_phase2(S, D, C):
    G8 = bcast_rows(S, C, "G8", 8, 0, 0.5)
    tiles = [dict(rows=128, hrow=i * 128) for i in range(16)]
    sbs = [tiles[0:6], tiles[6:12], tiles[12:16]]
    with S.phase():
        gfin = S.sb("gfin", [128, 1024])
        S.dma(gfin[:], D["gfin"][:])
        wd = S.sb("wd", [128, NFC, 1024], BF16)
        stg = [S.sb("stg%d" % i, [128, 8, 128]) for i in range(4)]
        wgb = [S.sb("wgb%d" % i, [128, 8, 128], BF16) for i in range(4)]
        wdsrc = D["f2_wd"].t.rearrange("(fc p) d -> p fc d", p=128)
        wdst = [S.sb("wdst%d" % i, [128, 1024]) for i in range(2)]
        for fc in range(NFC):
            S.dma(wdst[fc % 2][:], View(D["f2_wd"], wdsrc[:, fc, :]))
            S.copy("pool", wd[:, fc, :], wdst[fc % 2][:])
        u1 = S.sb("u1", [128, 8, 768], BF16)
        actT = S.sb("actT", [128, NFC, 768], BF16)
        xt = [S.sb("xt%d" % i, [128, 1024]) for i in range(2)]
        ht = [S.sb("ht%d" % i, [128, 1024]) for i in range(2)]
        ot = [S.sb("ot%d" % i, [128, 1024]) for i in range(2)]
        scr = (S.sb("ss", [128, 1]), S.sb("rs", [128, 1]), S.sb("junk", [128, 1024], BF16), S.sb("xn", [128, 1024], BF16))
        sg = [S.sb("sg%d" % i, [128, 512]) for i in range(2)]
        tmpd = [S.sb("tmpd%d" % i, [128, 512]) for i in range(2)]
        pst = S.ps("pst", [128, 8, 128], BF16)
        psg = [S.ps("psg%d" % i, [128, 512]) for i in range(2)]
        psu = [S.ps("psu%d" % i, [128, 512]) for i in range(2)]
        psd = [S.ps("psd%d" % i, [128, 512]) for i in range(2)]
        wgsrc = D["f2_wg"].t.rearrange("(kc p) f -> p kc f", p=128)
        wusrc = D["f2_wu"].t.rearrange("(kc p) f -> p kc f", p=128)
        nload = [0]
        xtn = [S.sb("xtn%d" % i, [128, 1024]) for i in range(2)]
        scr2 = (S.sb("ss2", [128, 1]), S.sb("rs2", [128, 1]), S.sb("junk2", [128, 1024], BF16), S.sb("xn2", [128, 1024], BF16))
        pst2 = S.ps("pst2", [128, 8, 128], BF16)

        def norm_tile(sb, ti):
            t = sb[ti]
            x = xtn[ti % 2]
            c0 = ti * 128
            S.dma(x[:], D["h2_sp"][t["hrow"]:t["hrow"] + 128, :])
            norm_to_T(S, C, x, 128, C.gs2, 6, 0, pst2, lambda dc, c0=c0: u1[:, dc, c0:c0 + 128], scr2)
            t["c0"] = c0

        def upgate(sb):
            ncol = 128 * len(sb)
            groups = [(g0, min(512, ncol - g0)) for g0 in range(0, ncol, 512)]
            it = 0
            for fc in range(NFC):
                k = nload[0] % 2
                nload[0] += 1
                S.dma(stg[2 * k][:], View(D["f2_wg"], wgsrc[:, :, fc * 128:(fc + 1) * 128]))
                S.dma(stg[2 * k + 1][:], View(D["f2_wu"], wusrc[:, :, fc * 128:(fc + 1) * 128]))
                S.copy("pool", wgb[2 * k][:], stg[2 * k][:])
                S.copy("pool", wgb[2 * k + 1][:], stg[2 * k + 1][:])
                for (g0, gn) in groups:
                    pg, pu = psg[it % 2], psu[it % 2]
                    for kc in range(8):
                        S.mm(pg[:, :gn], wgb[2 * k][:, kc, :], u1[:, kc, g0:g0 + gn], start=(kc == 0), stop=(kc == 7))
                    for kc in range(8):
                        S.mm(pu[:, :gn], wgb[2 * k + 1][:, kc, :], u1[:, kc, g0:g0 + gn], start=(kc == 0), stop=(kc == 7))
                    s = sg[it % 2]
                    S.act(s[:, :gn], pg[:, :gn], AF.Silu)
                    S.tt("dve", actT[:, fc, g0:g0 + gn], s[:, :gn], pu[:, :gn], ALU.mult)
                    it += 1
        def down_tile(sb, ti):
            if True:
                t = sb[ti]
                c0 = t["c0"]
                x = xt[ti % 2]
                h = ht[ti % 2]
                o = ot[ti % 2]
                S.dma(x[:], D["h2_sp"][t["hrow"]:t["hrow"] + 128, :])
                for hf in range(2):
                    for fc in range(NFC):
                        S.mm(psd[hf][:, :], actT[:, fc, c0:c0 + 128], wd[:, fc, hf * 512:(hf + 1) * 512],
                             start=(fc == 0), stop=(fc == NFC - 1))
                for hf in range(2):
                    S.tt("dve", tmpd[hf][:], psd[hf][:], G8[:, hf * 512:(hf + 1) * 512], ALU.mult)
                    S.tt("pool", h[:, hf * 512:(hf + 1) * 512], tmpd[hf][:], x[:, hf * 512:(hf + 1) * 512], ALU.add)
                ss, rs, junk, xn = scr
                S.memset("pool", ss[:], 0.0)
                S.act(junk[:], h[:], AF.Square, accum_out=ss[:])
                S.ts("dve", rs[:], ss[:], 1.0 / 1024, EPS, ALU.mult, ALU.add)
                S.act(rs[:], rs[:], AF.Sqrt)
                S.op("dve", lambda e: e.reciprocal(out=rs.t[:], in_=rs.t[:]), reads=[rs], writes=[rs])
                S.act(o[:], h[:], AF.Copy, scale=rs[:, 0:1])
                S.tt("pool", o[:], o[:], gfin[:], ALU.mult)
                S.dma(D["out"][t["hrow"]:t["hrow"] + 128, :], o[:])

        for ti in range(len(sbs[0])):
            norm_tile(sbs[0], ti)
        for k_, sb in enumerate(sbs):
            upgate(sb)
            nxt = sbs[k_ + 1] if k_ + 1 < len(sbs) else []
            for ti in range(max(len(sb), len(nxt))):
                if ti < len(sb):
                    down_tile(sb, ti)
                if ti < len(nxt):
                    norm_tile(nxt, ti)


SP = {"h_sp": ([2048, 1024], F32), "qT_sp": ([128, 4, 2048], BF16), "yret_sp": ([2048, 512], F32),
      "ulT_sp": ([128, 8, UC], BF16), "yext_sp": ([2, 32, 128, 512], F32), "bonusT_sp": ([128, 4, 2048], F32),
      "gT_sp": ([128, 4, 2048], F32), "modT_sp": ([128, 72, 2], F32)}
ST = {"st_ret": ([2, 2, 128, 4, 128], F32), "st_rwc": ([2, 128, 4, 64], F32), "st_rw": ([2, 128, 4, 128], F32)}
def build1():
    nc = bass.Bass("TRN2", target_bir_lowering=False)
    S = Sched(nc)
    D = {}
    for n, shp in IN_SHAPES.items():
        D[n] = S.dram(n, nc.dram_tensor(n, shp, F32, kind="ExternalInput").ap())
    for n, (shp, dt) in {**SP, **ST}.items():
        D[n] = S.dram(n, nc.dram_tensor(n, shp, dt, kind="ExternalOutput").ap())
    C = Ctx()
    mod_phase(S, D, C)
    S.dma(D["modT_sp"][:], C.modT[:])
    ffn_phase1(S, D, C)
    retention_phase1(S, D, C)
    rwkv_phase1(S, D, C)
    S.finish()
    return nc
P2_IN = ["ident", "gvec", "lgb", "rcst", "w_in"]
def build2():
    nc = bass.Bass("TRN2", target_bir_lowering=False)
    S = Sched(nc)
    D = {}
    for n in P2_IN:
        D[n] = S.dram(n, nc.dram_tensor(n, IN_SHAPES[n], F32, kind="ExternalInput").ap())
    for n, shp in P2_SHAPES.items():
        D[n] = S.dram(n, nc.dram_tensor(n, shp, F32, kind="ExternalInput").ap())
    for n, (shp, dt) in SP.items():
        D[n] = S.dram(n, nc.dram_tensor(n, shp, dt, kind="ExternalInput").ap())
    D["own_st_ret"] = S.dram("own_st_ret", nc.dram_tensor("own_st_ret", ST["st_ret"][0], F32, kind="ExternalInput").ap())
    D["own_st_rwc"] = S.dram("own_st_rwc", nc.dram_tensor("own_st_rwc", ST["st_rwc"][0], F32, kind="ExternalInput").ap())
    D["all_st_ret"] = S.dram("all_st_ret", nc.dram_tensor("all_st_ret", [8] + ST["st_ret"][0], F32, kind="ExternalInput").ap())
    D["all_st_rw"] = S.dram("all_st_rw", nc.dram_tensor("all_st_rw", [8] + ST["st_rw"][0], F32, kind="ExternalInput").ap())
    D["h2_sp"] = S.dram("h2_sp", nc.dram_tensor("h2_sp", [2048, 1024], F32, kind="ExternalOutput").ap())
    D["out"] = S.dram("out", nc.dram_tensor("out", [2048, 1024], F32, kind="ExternalOutput").ap())
    D["dbg"] = S.dram("dbg", nc.dram_tensor("dbg", [8, 128, 1024], F32, kind="ExternalOutput").ap())
    C = Ctx()
    mod_reload(S, D, C)
    merge_phase2(S, D, C)
    ffn_phase2(S, D, C)
    S.finish()
    return nc


def build_fused():
    nc = bass.Bass("TRN2", target_bir_lowering=False)
    S = Sched(nc)
    D = {}
    for n, shp in {**IN_SHAPES, **P2_SHAPES}.items():
        D[n] = S.dram(n, nc.dram_tensor(n, shp, F32, kind="ExternalInput").ap())
    for n, (shp, dt) in {**SP, **ST}.items():
        D[n] = S.dram(n, nc.dram_tensor(n, shp, dt).ap())
    D["h2_sp"] = S.dram("h2_sp", nc.dram_tensor("h2_sp", [2048, 1024], F32).ap())
    D["out"] = S.dram("out", nc.dram_tensor("out", [2048, 1024], F32, kind="ExternalOutput").ap())
    snd = nc.dram_tensor("st_snd", [512, 512], F32)
    rcv = nc.dram_tensor("st_rcv", [4096, 512], F32)
    dsnd, drcv = S.dram("st_snd", snd.ap()), S.dram("st_rcv", rcv.ap())
    C = Ctx()
    mod_phase(S, D, C)
    ffn_phase1(S, D, C)
    retention_phase1(S, D, C)
    rwkv_phase1(S, D, C)
    sv = snd.ap().rearrange("(x p) (h e) -> x p h e", p=128, h=4)
    for dr in range(2):
        S.dma(View(dsnd, sv[dr]), D["st_ret"][dr, 0])
        S.dma(View(dsnd, sv[2 + dr]), D["st_rw"][dr])
    S.collective(drcv, dsnd, snd.ap().opt(), rcv.ap().opt())
    rv = rcv.ap().rearrange("(r x o p) (h e) -> r x o p h e", r=8, x=4, o=1, p=128, h=4)
    D["all_st_ret"] = S.dram("all_st_ret", rv)
    D["all_st_rw"] = S.dram("all_st_rw", rv[:, 2:4, 0])
    for n in ("all_st_ret", "all_st_rw"):
        D[n].last_w = drcv.last_w
    D["own_st_ret"] = D["st_ret"]
    D["own_st_rwc"] = D["st_rwc"]
    merge_phase2(S, D, C)
    ffn_phase2(S, D, C)
    S.finish()
    return nc


_CACHE = {}


def kernel(**inp):
    inp = {k: np.asarray(v) for k, v in inp.items()}
    if "nc" not in _CACHE:
        _CACHE["nc"] = build_fused()
    nc = _CACHE["nc"]
    com = {**prep_common(inp), **prep_ret_common(inp), **prep_rw_common(inp), **prep_p2_common(inp)}
    maps = [{**com, **prep_core(inp, r), **prep_ret_core(inp, r), **prep_p2_core(r)} for r in range(8)]
    res = run_bass_kernel_spmd(nc, maps, core_ids=list(range(8))).results
    out = np.stack([r_["out"] for r_ in res]).reshape(2, 8192, 1024)
    return np.ascontiguousarray(out.astype(np.float32))
```
